# Optimizing a Trainium2 kernel written in Bass

```python
import numpy as np
import jax, jax.numpy as jnp
from jax import lax

D_MODEL = 1024
BATCH = 8
SEQ = 2048
DEPTH = 2

N_MIXERS = 4
MIX_W = D_MODEL // N_MIXERS
HEAD_DIM = 64
N_Q_HEADS = MIX_W // HEAD_DIM
N_KV_HEADS = 2
Q_PER_KV = N_Q_HEADS // N_KV_HEADS
KV_W = N_KV_HEADS * HEAD_DIM
GMLP_GROUPS = 4
GMLP_CHUNK = 128
CMP_LEN = 32
CMP_STRIDE = 16
CMP_HIDDEN = 128
SEL_BLOCK = 64
SEL_TOPK = 8
N_LOCAL_BLOCKS = 2
WINDOW = 512
Q_BLOCK = 128
CONF_KERNEL = 31
SCONV_KERNEL = 3
D_FF = 4 * D_MODEL
NSA_GATE_W = N_Q_HEADS * 3
IN_WIDTHS = (MIX_W, MIX_W, MIX_W, KV_W, KV_W, KV_W, KV_W, KV_W, KV_W, NSA_GATE_W, MIX_W, MIX_W, MIX_W, MIX_W, MIX_W)
IN_COLS = sum(IN_WIDTHS)
NEG_INF = -1e30

kernel_name = "hybrid_gated_nsa_gmlp_conv_block"


def _rms(x, g, eps=1e-6):
    xf = x.astype(jnp.float32)
    y = xf * lax.rsqrt(jnp.mean(xf * xf, axis=-1, keepdims=True) + eps)
    return (y * g.astype(jnp.float32)).astype(x.dtype)


def _layernorm(x, g, b, eps=1e-5):
    xf = x.astype(jnp.float32)
    mu = jnp.mean(xf, axis=-1, keepdims=True)
    var = jnp.mean(jnp.square(xf - mu), axis=-1, keepdims=True)
    return ((xf - mu) * lax.rsqrt(var + eps) * g.astype(jnp.float32) + b.astype(jnp.float32)).astype(x.dtype)


def _causal_depthwise_conv(x, w):
    k = w.shape[0]
    return lax.conv_general_dilated(x, w[:, None, :].astype(x.dtype), window_strides=(1,),
                                    padding=[(k - 1, 0)], dimension_numbers=('NWC', 'WIO', 'NWC'),
                                    feature_group_count=x.shape[-1])


def _gmlp_spatial_gate(u, v, ln_g, ln_b, ws, bs):
    B, S, _ = v.shape
    nc = S // GMLP_CHUNK
    v = _layernorm(v, ln_g, ln_b).reshape(B, nc, GMLP_CHUNK, GMLP_GROUPS, MIX_W // GMLP_GROUPS)
    mask = np.tril(np.ones((GMLP_CHUNK, GMLP_CHUNK), dtype=bool))
    wsm = jnp.where(mask, ws, 0).astype(v.dtype)
    mixed = jnp.einsum('gts,bnsgc->bntgc', wsm, v) + bs.T[None, None, :, :, None].astype(v.dtype)
    return u * mixed.reshape(B, S, MIX_W)


def _overlap_matrix(n_cmp, n_sel):
    cs = np.arange(n_cmp) * CMP_STRIDE
    ce = cs + CMP_LEN
    ss = np.arange(n_sel) * SEL_BLOCK
    se = ss + SEL_BLOCK
    return ((cs[:, None] < se[None, :]) & (ce[:, None] > ss[None, :])).astype(np.float32)


def _nsa(q, kc, vc, ks, vs, kw, vw, gate_logits, q_g, k_g, cmp_pe, cmp_w1, cmp_w2):
    B, S, _ = q.shape
    T = Q_BLOCK
    nqb = S // T
    pos = jnp.arange(S)
    scale = HEAD_DIM ** -0.5
    heads = lambda t: t.reshape(B, S, N_KV_HEADS, HEAD_DIM)
    q = _rms(q.reshape(B, S, N_KV_HEADS, Q_PER_KV, HEAD_DIM), q_g)

    n_cmp = (S - CMP_LEN) // CMP_STRIDE + 1
    cmp_idx = np.arange(n_cmp)[:, None] * CMP_STRIDE + np.arange(CMP_LEN)[None, :]

    def compress(t, pe, w1, w2):
        blk = jnp.take(heads(t), cmp_idx, axis=1) + pe[None, None, :, None, :]
        blk = jnp.swapaxes(blk, 2, 3).reshape(B, n_cmp, N_KV_HEADS, CMP_LEN * HEAD_DIM)
        hid = jax.nn.gelu(jnp.einsum('bnhf,fe->bnhe', blk, w1))
        return jnp.einsum('bnhe,ed->bnhd', hid, w2)

    k_cmp = _rms(compress(kc, cmp_pe[0], cmp_w1[0], cmp_w2[0]), k_g[0])
    v_cmp = compress(vc, cmp_pe[1], cmp_w1[1], cmp_w2[1])
    cmp_valid = cmp_idx[:, -1][None, :] <= np.arange(S)[:, None]
    s = jnp.einsum('bshgd,bnhd->bhgsn', q, k_cmp).astype(jnp.float32) * scale
    p_cmp = jnp.where(cmp_valid, jax.nn.softmax(jnp.where(cmp_valid, s, NEG_INF), axis=-1), 0.0)
    o_cmp = jnp.einsum('bhgsn,bnhd->bshgd', p_cmp.astype(v_cmp.dtype), v_cmp)

    n_sel = S // SEL_BLOCK
    k_top = min(SEL_TOPK, n_sel)
    p_slc = jnp.einsum('bhgsn,nj->bhsj', p_cmp, jnp.asarray(_overlap_matrix(n_cmp, n_sel)))
    blk = np.arange(n_sel)[None, :]
    cur = (np.arange(S) // SEL_BLOCK)[:, None]
    forced = (blk == 0) | ((cur - blk >= 0) & (cur - blk < N_LOCAL_BLOCKS))
    causal = blk <= cur
    score = jnp.where(forced, jnp.inf, jnp.where(causal, p_slc, -jnp.inf))
    sel_idx = lax.top_k(score, k_top)[1]

    ks_h = jnp.swapaxes(_rms(heads(ks), k_g[1]), 1, 2)
    vs_h = jnp.swapaxes(heads(vs), 1, 2)
    gather = jax.vmap(jax.vmap(lambda a, i: a[i]))
    tok_off = jnp.arange(SEL_BLOCK)

    def sel_block(args):
        qb, ib, pb = args
        tok = (ib[..., None] * SEL_BLOCK + tok_off).reshape(B, N_KV_HEADS, T, k_top * SEL_BLOCK)
        flat = tok.reshape(B, N_KV_HEADS, -1)
        kg = gather(ks_h, flat).reshape(B, N_KV_HEADS, T, k_top * SEL_BLOCK, HEAD_DIM)
        vg = gather(vs_h, flat).reshape(B, N_KV_HEADS, T, k_top * SEL_BLOCK, HEAD_DIM)
        sb = jnp.einsum('bthgd,bhtkd->bhgtk', qb, kg).astype(jnp.float32) * scale
        m = (tok <= pb[:, None])[:, :, None]
        pr = jax.nn.softmax(jnp.where(m, sb, NEG_INF), axis=-1).astype(vg.dtype)
        return jnp.einsum('bhgtk,bhtkd->bthgd', pr, vg)

    q_blocks = jnp.moveaxis(q.reshape(B, nqb, T, N_KV_HEADS, Q_PER_KV, HEAD_DIM), 1, 0)
    i_blocks = jnp.moveaxis(sel_idx.reshape(B, N_KV_HEADS, nqb, T, k_top), 2, 0)
    p_blocks = pos.reshape(nqb, T)
    o_sel = jnp.moveaxis(lax.map(sel_block, (q_blocks, i_blocks, p_blocks)), 0, 1)
    o_sel = o_sel.reshape(B, S, N_KV_HEADS, Q_PER_KV, HEAD_DIM)

    kw_h = _rms(heads(kw), k_g[2])
    vw_h = heads(vw)
    win_idx = np.arange(nqb)[:, None] * T + np.arange(WINDOW + T)[None, :]
    kpos = (win_idx - WINDOW)[:, None, :]
    qpos = np.arange(S).reshape(nqb, T)[:, :, None]
    wmask = (kpos <= qpos) & (kpos > qpos - WINDOW) & (kpos >= 0)
    pad = ((0, 0), (WINDOW, 0), (0, 0), (0, 0))
    kb = jnp.take(jnp.pad(kw_h, pad), win_idx, axis=1)
    vb = jnp.take(jnp.pad(vw_h, pad), win_idx, axis=1)
    qb = q.reshape(B, nqb, T, N_KV_HEADS, Q_PER_KV, HEAD_DIM)
    sw = jnp.einsum('bnthgd,bnmhd->bnhgtm', qb, kb).astype(jnp.float32) * scale
    pw = jax.nn.softmax(jnp.where(wmask[:, None, None], sw, NEG_INF), axis=-1).astype(vb.dtype)
    o_win = jnp.einsum('bnhgtm,bnmhd->bnthgd', pw, vb).reshape(B, S, N_KV_HEADS, Q_PER_KV, HEAD_DIM)

    g = jax.nn.sigmoid(gate_logits.reshape(B, S, N_KV_HEADS, Q_PER_KV, 3))
    o = g[..., 0:1] * o_cmp + g[..., 1:2] * o_sel + g[..., 2:3] * o_win
    return o.reshape(B, S, MIX_W)


def setup_inputs(seed: int = 0) -> dict:
    key = jax.random.key(seed)
    ks = jax.random.split(key, 26)
    L = DEPTH
    nrm = lambda k, shape, sc: jax.random.normal(k, shape, jnp.float32) * sc
    res_sc = (2.0 * DEPTH) ** -0.5
    return {
        "x": nrm(ks[0], (BATCH, SEQ, D_MODEL), 1.0),
        "norm1_g": 1.0 + nrm(ks[1], (L, D_MODEL), 0.02),
        "w_in": nrm(ks[2], (L, D_MODEL, IN_COLS), D_MODEL ** -0.5),
        "gmlp_ln_g": 1.0 + nrm(ks[3], (L, MIX_W), 0.02),
        "gmlp_ln_b": nrm(ks[4], (L, MIX_W), 0.02),
        "gmlp_ws": nrm(ks[5], (L, GMLP_GROUPS, GMLP_CHUNK, GMLP_CHUNK), GMLP_CHUNK ** -0.5),
        "gmlp_bs": 1.0 + nrm(ks[6], (L, GMLP_GROUPS, GMLP_CHUNK), 0.02),
        "nsa_q_norm_g": 1.0 + nrm(ks[7], (L, HEAD_DIM), 0.02),
        "nsa_k_norm_g": 1.0 + nrm(ks[8], (L, 3, HEAD_DIM), 0.02),
        "nsa_cmp_pe": nrm(ks[9], (L, 2, CMP_LEN, HEAD_DIM), 0.02),
        "nsa_cmp_w1": nrm(ks[10], (L, 2, CMP_LEN * HEAD_DIM, CMP_HIDDEN), (CMP_LEN * HEAD_DIM) ** -0.5),
        "nsa_cmp_w2": nrm(ks[11], (L, 2, CMP_HIDDEN, HEAD_DIM), CMP_HIDDEN ** -0.5),
        "conf_conv_w": nrm(ks[12], (L, CONF_KERNEL, MIX_W), CONF_KERNEL ** -0.5),
        "conf_conv_b": nrm(ks[13], (L, MIX_W), 0.02),
        "conf_ln_g": 1.0 + nrm(ks[14], (L, MIX_W), 0.02),
        "conf_ln_b": nrm(ks[15], (L, MIX_W), 0.02),
        "sconv_w": nrm(ks[16], (L, SCONV_KERNEL, MIX_W), SCONV_KERNEL ** -0.5),
        "w_branch": nrm(ks[17], (L, N_MIXERS, MIX_W, D_MODEL), MIX_W ** -0.5),
        "w_gate": nrm(ks[18], (L, D_MODEL, N_MIXERS * D_MODEL), D_MODEL ** -0.5),
        "b_gate": nrm(ks[19], (L, N_MIXERS * D_MODEL), 0.02),
        "w_out": nrm(ks[20], (L, D_MODEL, D_MODEL), D_MODEL ** -0.5 * res_sc),
        "norm2_g": 1.0 + nrm(ks[21], (L, D_MODEL), 0.02),
        "w_mlp1": nrm(ks[22], (L, D_MODEL, D_FF), D_MODEL ** -0.5),
        "w_mlp2": nrm(ks[23], (L, D_FF, D_MODEL), D_FF ** -0.5 * res_sc),
    }


def reference(x, norm1_g, w_in, gmlp_ln_g, gmlp_ln_b, gmlp_ws, gmlp_bs, nsa_q_norm_g, nsa_k_norm_g,
              nsa_cmp_pe, nsa_cmp_w1, nsa_cmp_w2, conf_conv_w, conf_conv_b, conf_ln_g, conf_ln_b,
              sconv_w, w_branch, w_gate, b_gate, w_out, norm2_g, w_mlp1, w_mlp2):
    B, S, D = x.shape
    split_points = np.cumsum(IN_WIDTHS)[:-1].tolist()
    for l in range(DEPTH):
        xn = _rms(x, norm1_g[l])
        (gu, gv, q, kc, vc, ks_, vs_, kw, vw, ng, ca, cb, sB, sC, sh) = jnp.split(
            jnp.einsum('bsd,de->bse', xn, w_in[l]), split_points, axis=-1)
        y_a = _gmlp_spatial_gate(jax.nn.gelu(gu), jax.nn.gelu(gv), gmlp_ln_g[l], gmlp_ln_b[l], gmlp_ws[l], gmlp_bs[l])
        y_b = _nsa(q, kc, vc, ks_, vs_, kw, vw, ng, nsa_q_norm_g[l], nsa_k_norm_g[l],
                   nsa_cmp_pe[l], nsa_cmp_w1[l], nsa_cmp_w2[l])
        z = ca * jax.nn.sigmoid(cb)
        z = _causal_depthwise_conv(z, conf_conv_w[l]) + conf_conv_b[l]
        y_c = jax.nn.silu(_layernorm(z, conf_ln_g[l], conf_ln_b[l]))
        y_d = sB * _causal_depthwise_conv(sC * sh, sconv_w[l])
        ys = jnp.stack([y_a, y_b, y_c, y_d], axis=2)
        proj = jnp.einsum('bsnc,ncd->bsnd', ys, w_branch[l])
        gates = jax.nn.sigmoid(jnp.einsum('bsd,de->bse', xn, w_gate[l]) + b_gate[l]).reshape(B, S, N_MIXERS, D)
        mixed = jnp.sum(gates * proj, axis=2)
        x = x + jnp.einsum('bsd,de->bse', mixed, w_out[l])
        hn = _rms(x, norm2_g[l])
        hid = jnp.square(jax.nn.relu(jnp.einsum('bsd,df->bsf', hn, w_mlp1[l])))
        x = x + jnp.einsum('bsf,fd->bsd', hid, w_mlp2[l])
    return x
```

```python
import contextlib
import numpy as np
import ml_dtypes
import concourse.bass as bass
import concourse.mybir as mybir
from concourse.bass_utils import run_bass_kernel_spmd

F32 = mybir.dt.float32
BF16 = mybir.dt.bfloat16
AF = mybir.ActivationFunctionType
ALU = mybir.AluOpType
AX = mybir.AxisListType

L = 2
S = 2048
D = 1024
NT = 16
NPP = 706
NEGB = -30000.0
BIS = 0
DEBUG_STOP = None
N_LAYERS = L

C_G1, C_G2, C_BS, C_GQ, C_GK, C_CW, C_CB, C_CLG, C_CLB, C_SW, C_BG, C_PE, C_LNG, C_LNB = \
    0, 8, 16, 20, 21, 24, 86, 88, 90, 92, 98, 130, 194, 450
W_A, W_BT1, W_BT2, W_BF, W_C, W_D = 0, 512, 1024, 1292, 1548, 2060


class TK:
    ENGS = ("pe", "act", "dve", "pool", "sp")
    NDS = 16

    def __init__(self):
        self.cnt = {e: 0 for e in self.ENGS}
        self.ops = {e: [] for e in self.ENGS}
        self.lw = {}
        self.rd = {}
        self.waited = {e: {} for e in self.ENGS}
        self.dval = [0] * self.NDS
        self.dnext = 0
        self.dnext_g = 0
        self.pending = {e: {} for e in self.ENGS}

    def _collect(self, eng, reads, writes):
        deps = {}

        def add(tok):
            sid, v = tok
            if deps.get(sid, 0) < v:
                deps[sid] = v
        for k in reads:
            t = self.lw.get(k)
            if t is not None:
                if t[0] == eng and eng == "pe":
                    continue
                add(t)
        for k in writes:
            t = self.lw.get(k)
            if t is not None and not (t[0] == eng and eng == "pe"):
                add(t)
            for sid, v in self.rd.get(k, {}).items():
                if not (sid == eng and eng == "pe"):
                    add((sid, v))
        for sid, v in self.pending[eng].items():
            add((sid, v))
        self.pending[eng] = {}
        waits = []
        w = self.waited[eng]
        for sid, v in deps.items():
            if w.get(sid, 0) < v:
                w[sid] = v
                waits.append((sid, v))
        return waits

    def _commit(self, tok, reads, writes):
        sid, v = tok
        for k in reads:
            d = self.rd.setdefault(k, {})
            if d.get(sid, 0) < v:
                d[sid] = v
        for k in writes:
            self.lw[k] = tok
            self.rd[k] = {}

    def op(self, eng, fn, reads=(), writes=()):
        writes = list(writes) + [k for k in reads if k[0] == "ps"]
        reads = [k for k in reads if k[0] != "ps"]
        waits = self._collect(eng, reads, writes)
        self.cnt[eng] += 1
        tok = (eng, self.cnt[eng])
        self.ops[eng].append((waits, fn, (eng, 1), None))
        self._commit(tok, reads, writes)

    def dma(self, eng, fn, n, reads=(), writes=()):
        half = self.NDS // 2
        if eng == "pool":
            j = half + self.dnext_g
            self.dnext_g = (self.dnext_g + 1) % half
        else:
            j = self.dnext
            self.dnext = (self.dnext + 1) % half
        sid = "d%d" % j
        waits = self._collect(eng, reads, writes)
        v0 = self.dval[j]
        if self.waited[eng].get(sid, 0) < v0:
            self.waited[eng][sid] = v0
            waits.append((sid, v0))
        self.dval[j] = v0 + 16 * n
        tok = (sid, self.dval[j])
        self.ops[eng].append((waits, fn, None, sid))
        self._commit(tok, reads, writes)
        return tok

    def barrier(self):
        toks = {e: self.cnt[e] for e in ("pe", "act", "dve", "pool")}
        for j in range(self.NDS):
            toks["d%d" % j] = self.dval[j]
        for e in self.ENGS:
            for sid, v in toks.items():
                if v > 0 and sid != e:
                    if self.pending[e].get(sid, 0) < v:
                        self.pending[e][sid] = v

    def final_tokens(self):
        toks = {e: self.cnt[e] for e in ("pe", "act", "dve", "pool")}
        for j in range(self.NDS):
            toks["d%d" % j] = self.dval[j]
        return toks


def build_nc(n_layers=L, debug_stop=None):
    nc = bass.Bass("TRN2", target_bir_lowering=False)
    T = TK()

    def din(name, shape, dt):
        return nc.dram_tensor(name, list(shape), dt, kind="ExternalInput").ap()

    x_d = din("x", [S, D], F32)
    pp_d = din("pp", [128, L * NPP], F32)
    cb_d = din("cb16", [128, 2560], BF16)
    eall_d = din("eall", [2, 128, S], BF16)
    rc_d = din("rconst", [127, 33], BF16)
    fb_d = din("fbias", [128, NT * 32], F32)
    win_d = din("win", [L, 128, 8, 2828], F32)
    wg_d = din("wg", [L, 8, 128, 8 * 4 * 128], F32)
    wb_d = din("wb", [L, 8, 128, 8 * 128], F32)
    wo_d = din("wo", [L, 128, 8, 1024], F32)
    w1_d = din("w1", [L, 4, 128, 8, 1024], F32)
    w2_d = din("w2", [L, 4, 128, 8, 1024], F32)
    cw1_d = din("cw1", [L, 2, 128, 32, 128], F32)
    cw2_d = din("cw2", [L, 128, 128], F32)
    ws_d = din("ws", [L, 128, 512], F32)
    out_d = nc.dram_tensor("out", [S, D], F32, kind="ExternalOutput").ap()
    dbgy_d = None
    if debug_stop is not None:
        dbgy_d = nc.dram_tensor("dbgy", [128, 8 * S], BF16, kind="ExternalOutput").ap()

    es = contextlib.ExitStack()
    with es:
        def sbt(name, shape, dt):
            return es.enter_context(nc.sbuf_tensor(name, list(shape), dt))

        Xt = sbt("X", [128, NT * D], F32)
        XNt = sbt("XN", [128, 8 * S], BF16)
        Yt = sbt("Y", [128, 8 * S], BF16)
        Rt = sbt("R", [128, 30720], BF16)
        ppt = sbt("ppt", [128, L * NPP], F32)
        cbt = sbt("cbt", [128, 2560], BF16)
        fbt = sbt("fbt", [128, NT * 32], F32)
        Rht = sbt("Rh", [128, 2 * 97], BF16)
        KCt = sbt("KC", [128, 2 * 128], BF16)
        smt = sbt("small", [128, 256], F32)
        onest = sbt("ones256", [128, 128], F32)
        wsTt = sbt("wsT", [128, 512], BF16)
        nbtt = sbt("nbt", [128, 128], BF16)
        nbt2t = sbt("nbt2", [128, 128], BF16)
        sm2t = sbt("small2", [128, 256], F32)
        pst = [es.enter_context(nc.psum_tensor("ps%d" % i, [128, 512], F32)) for i in range(8)]
        sems = {}
        for e in TK.ENGS:
            sems[e] = es.enter_context(nc.semaphore("s_" + e))
        for j in range(TK.NDS):
            sems["d%d" % j] = es.enter_context(nc.semaphore("s_d%d" % j))

        X = Xt[:].rearrange("p (t d) -> p t d", t=NT)
        XN = XNt[:].rearrange("p (k s) -> p k s", k=8)
        Y = Yt[:].rearrange("p (k s) -> p k s", k=8)
        pp = ppt[:]
        cb = cbt[:]
        ident = cb[:, 0:128]
        M_le = cb[:, 128:256]
        M_ge = cb[:, 256:384]
        M_gt = cb[:, 384:512]
        cmpmask = cb[:, 512:2560]
        fb = fbt[:].rearrange("p (t j) -> p t j", t=NT)
        Rh = Rht[:].rearrange("p (h c) -> p h c", h=2)
        KC = KCt[:].rearrange("p (h c) -> p h c", h=2)
        sm = smt[:]
        ones256 = onest[:]
        wsT = wsTt[:].rearrange("p (g t) -> p g t", g=4)
        nbt = nbtt[:]
        sm2 = sm2t[:]
        c_nb = [nbtt[:], nbt2t[:]]
        c_rs = [sm2[:, 0:4], sm2[:, 4:8]]
        c_ri = [sm2[:, 8:12], sm2[:, 12:16]]
        c_cf = [sm2[:, 16:20], sm2[:, 20:24]]
        c_t8 = [[sm2[:, 32:40], sm2[:, 40:48]], [sm2[:, 48:56], sm2[:, 56:64]]]
        c_sc = [[sm2[:, 64:96], sm2[:, 96:128]], [sm2[:, 128:160], sm2[:, 160:192]]]

        def ps(i):
            return pst[i][:]

        def psb(i):
            return pst[i][:].bitcast(BF16)

        def R(off, nel, dt):
            if dt == BF16:
                return Rt[:, off // 2: off // 2 + nel]
            return Rt[:, off // 2: off // 2 + 2 * nel].bitcast(F32)

        def PK(i):
            return ("ps", i)

        def xnk(c):
            return [("XN", t) for t in range(4 * c, 4 * c + 4)]

        def yk(slot, c):
            return [("Y", slot, t) for t in range(4 * c, 4 * c + 4)]

        rots = {}

        def rot(name, items):
            i = rots.get(name, 0)
            rots[name] = i + 1
            return items[i % len(items)]

        def mmg(out, pairs, reads, writes):
            def fn(e):
                n = len(pairs)
                ins = None
                for i, (l, r) in enumerate(pairs):
                    ins = e.matmul(out, l, r, start=(i == 0), stop=(i == n - 1))
                return ins
            T.op("pe", fn, reads, writes)

        def mm1(out, l, r, start, stop, reads, writes, skip=False):
            T.op("pe", lambda e: e.matmul(out, l, r, start=start, stop=stop, skip_group_check=skip), reads, writes)

        def tps(outs_ins, idn, reads, writes):
            def fn(e):
                ins = None
                for o, i in outs_ins:
                    ins = e.transpose(o, i, idn)
                return ins
            T.op("pe", fn, reads, writes)

        def act(out, in_, func, reads, writes, bias=None, scale=None, accum=None):
            kw = {}
            if bias is not None:
                kw["bias"] = bias
            if scale is not None:
                kw["scale"] = scale
            if accum is not None:
                kw["accum_out"] = accum
            T.op("act", lambda e: e.activation(out, in_, func, **kw), reads, writes)

        def ts(out, in0, s1, s2, op0, op1, reads, writes, eng="dve"):
            if op1 is None:
                T.op(eng, lambda e: e.tensor_scalar(out, in0, s1, None, op0), reads, writes)
            else:
                T.op(eng, lambda e: e.tensor_scalar(out, in0, s1, s2, op0, op1), reads, writes)

        def tt(out, in0, in1, op, reads, writes, eng="dve"):
            T.op(eng, lambda e: e.tensor_tensor(out, in0, in1, op), reads, writes)

        def stt(out, in0, sc, in1, op0, op1, reads, writes):
            T.op("dve", lambda e: e.scalar_tensor_tensor(out, in0, sc, in1, op0, op1), reads, writes)

        def cpy(out, in_, reads, writes, eng="dve"):
            if eng == "act":
                act(out, in_, AF.Copy, reads, writes)
            else:
                T.op(eng, lambda e: e.tensor_copy(out, in_), reads, writes)

        def recip(out, in_, reads, writes):
            T.op("dve", lambda e: e.reciprocal(out, in_), reads, writes)

        def mset(ap, val, writes, eng="dve"):
            T.op(eng, lambda e: e.memset(ap, val), (), writes)

        def dma(eng, out, in_, reads, writes):
            def fn(e, sem):
                e.dma_start(out=out, in_=in_).then_inc(sem, 16)
            T.dma(eng, fn, 1, reads, writes)

        def wload(out, in_, writes):
            dma("pool", out, in_, (), writes)

        def wload_flat(off, nel, in_flat, writes):
            o = R(off, nel, BF16)
            if nel > 2048:
                o = o.rearrange("p (a b) -> p a b", b=2048)
                in_flat = in_flat.rearrange("p (a b) -> p a b", b=2048)
            dma("pool", o, in_flat, (), writes)

        dma("sp", ppt[:], pp_d[:, :], (), [("pp",)])
        dma("sp", cbt[:], cb_d[:, :], (), [("cb",)])
        dma("sp", fbt[:], fb_d[:, :], (), [("fb",)])
        for h in range(2):
            dma("sp", Rh[0:127, h, 64:97], rc_d[:, :], (), [("Rh", h)])
        for t in range(NT):
            dma("sp", X[:, t, :], x_d[t * 128:(t + 1) * 128, :], (), [("X", t)])
        mset(ones256, 1.0 / 256.0, [("ones",)])
        mset(KCt[:], 0.0, [("KC", 0), ("KC", 1)])
        mset(nbt, 0.0, [("cnb", 0)])
        mset(nbt2t[:], 0.0, [("cnb", 1)])

        def store_dbg():
            T.barrier()
            tok = None
            for t in range(NT):
                dma("sp", out_d[t * 128:(t + 1) * 128, :], X[:, t, :], [("X", t)], [("out", t)])
            dma("sp", dbgy_d[:, :], Yt[:], [("Y", s_, t) for s_ in range(8) for t in range(NT)], [("dbgy",)])

        def norm_phase(l, gcol):
            o_xs = [53248, 55296]
            o_junk = 57344
            junk = R(o_junk, 1024, BF16)
            ss = sm[:, 0:16]
            sd = sm[:, 16:32]
            rstd = sm[:, 32:48]
            for t in range(NT):
                act(junk, X[:, t, :], AF.Square, [("X", t)], [("junk",), ("ss", t)], accum=ss[:, t:t + 1])
            act(sd, ss, AF.Sqrt, [("ss", t) for t in range(NT)], [("sd",)], bias=1e-6, scale=1.0 / D)
            recip(rstd, sd, [("sd",)], [("rstd",)])
            g = pp[:, l * NPP + gcol: l * NPP + gcol + 8]
            for t in range(NT):
                xs = R(o_xs[t % 2], 1024, BF16)
                bk = 6 + (t % 2)
                act(xs, X[:, t, :], AF.Copy, [("X", t), ("rstd",)], [("xs", t % 2)], scale=rstd[:, t:t + 1])
                pv = psb(bk).rearrange("p (k s) -> p k s", k=8)
                tps([(pv[:, k, :], xs[:, k * 128:(k + 1) * 128]) for k in range(8)], ident,
                    [("xs", t % 2), ("cb",)], [PK(bk)])
                tt(XN[:, :, t * 128:(t + 1) * 128], pv, g.unsqueeze(2).broadcast_to([128, 8, 128]), ALU.mult,
                   [PK(bk), ("pp",)], [("XN", t)])

        def nsa_compress(l):
            T.barrier()
            wbuf0 = R(0, 4096, BF16).rearrange("p (k c) -> p k c", k=8)
            o_kcsb, o_X, o_w1 = 16384, 24576, 32768
            kcsb = R(o_kcsb, 2048, BF16)
            peb = R(o_X, 64, BF16)
            cbh = sm[:, 54:56]
            w1sb = R(o_w1, 4096, BF16).rearrange("p (l e) -> p l e", l=32)
            hid = [R(40960 + 256 * h, 127, BF16) for h in range(2)]
            ktok = R(41472, 128, BF16)
            w2sb = R(41728, 128, BF16).rearrange("p (k c) -> p k c", k=2)
            junkc = R(41984, 64, F32)
            kss = sm[:, 48:50]
            ksd = sm[:, 50:52]
            krs = sm[:, 52:54]
            wload(w2sb, cw2_d[l, :, :].rearrange("p (k c) -> p k c", k=2), [("w2sb",)])
            for kv in range(2):
                wload_flat(o_w1, 4096, cw1_d[l, kv, :, :, :].rearrange("p l e -> p (l e)"), [("w1sb",)])
                for c in range(4):
                    bk = rot("b2a", [0, 1])
                    mmg(ps(bk), [(wbuf0[:, k, kv * 128:(kv + 1) * 128], XN[:, k, c * 512:(c + 1) * 512]) for k in range(8)],
                        [("wbuf", 0)] + xnk(c), [PK(bk)])
                    cpy(kcsb[:, c * 512:(c + 1) * 512], ps(bk), [PK(bk)], [("kcsb", c)], eng="act")
                if kv == 1:
                    wload(wbuf0, win_d[l, :, :, W_BT1:W_BT1 + 512], [("wbuf", 0)])
                if kv == 0:
                    cpy(peb, pp[:, l * NPP + C_PE: l * NPP + C_PE + 64], [("pp",)], [("peb",)], eng="dve")
                for h in range(2):
                    mmg(ps(7)[:, h:h + 1], [(w1sb[h * 64:(h + 1) * 64, l_, :], peb[h * 64:(h + 1) * 64, kv * 32 + l_: kv * 32 + l_ + 1])
                                           for l_ in range(32)], [("w1sb",), ("peb",)], [PK(7)])
                    cpy(cbh[:, h:h + 1], ps(7)[:, h:h + 1], [PK(7)], [("cbh", h)], eng="dve")
                for h in range(2):
                    bh = 2 + h
                    mmg(ps(bh)[:, 0:127], [(w1sb[h * 64:(h + 1) * 64, l_, :], kcsb[h * 64:(h + 1) * 64, l_: l_ + 2017: 16]) for l_ in range(32)],
                        [("w1sb",)] + [("kcsb", c) for c in range(4)], [PK(bh)])
                    act(hid[h], ps(bh)[:, 0:127], AF.Gelu_apprx_tanh, [PK(bh), ("cbh", h)], [("hid", h)], bias=cbh[:, h:h + 1], scale=1.0)
                    bq = 4 + h
                    mm1(ps(bq)[0:127, 0:64], hid[h], w2sb[:, kv, :], True, True, [("hid", h), ("w2sb",)], [PK(bq)])
                    if kv == 0:
                        act(junkc[0:127, :], ps(bq)[0:127, 0:64], AF.Square, [PK(bq)], [("junkc",), ("kss", h)],
                            accum=kss[0:127, h:h + 1])
                    else:
                        cpy(Rh[0:127, h, 0:64], ps(bq)[0:127, 0:64], [PK(bq)], [("Rh", h)], eng="act")
                if kv == 0:
                    act(ksd[0:127, :], kss[0:127, :], AF.Sqrt, [("kss", 0), ("kss", 1)], [("ksd",)], bias=1e-6, scale=1.0 / 64)
                    recip(krs[0:127, :], ksd[0:127, :], [("ksd",)], [("krs",)])
                    for h in range(2):
                        ts(ktok[0:127, h * 64:(h + 1) * 64], ps(4 + h)[0:127, 0:64], krs[0:127, h:h + 1], None, ALU.mult, None,
                           [PK(4 + h), ("krs",)], [("ktok",)])
                    pT = psb(6)
                    tps([(pT[:, 0:127], ktok[0:127, :])], ident[0:127, 0:127], [("ktok",), ("cb",)], [PK(6)])
                    gk = pp[:, l * NPP + C_GK: l * NPP + C_GK + 1]
                    ts(KC[0:64, 0, 0:127], pT[0:64, 0:127], gk[0:64, :], None, ALU.mult, None, [PK(6), ("pp",)], [("KC", 0)])
                    ts(KC[64:128, 1, 0:127], pT[64:128, 0:127], gk[64:128, :], None, ALU.mult, None, [PK(6), ("pp",)], [("KC", 1)])

        QSLOT = [0, 1, 4, 5]
        o_KW = [16384, 20480]
        o_vs, o_vw = 24576, 28800
        o_gs = 41984
        o_E = [42752, 43776, 44800, 49920 + 4096, 51968, 52992, 55040, 56064]
        o_yb = 45824
        o_sq, o_qn = 49920, 51968

        def KWt(h):
            return R(o_KW[h], 2048, BF16)

        def vaug(o):
            return R(o, NT * 2 * 65, BF16).rearrange("p (t h c) -> p t h c", t=NT, h=2)

        def nsa_proj(l):
            T.barrier()
            wbuf0 = R(0, 4096, BF16).rearrange("p (k c) -> p k c", k=8)
            wbuf1 = R(8192, 4096, BF16).rearrange("p (k c) -> p k c", k=8)
            vs_, vw_ = vaug(o_vs), vaug(o_vw)
            gs = R(o_gs, 192, F32)
            sq = R(o_sq, 512, F32)
            qn = R(o_qn, 512, BF16)
            for h in range(2):
                dma("sp", Y[:, 6 + h, :], eall_d[h, :, :], (), [("Y", 6 + h, t) for t in range(NT)])
                mset(KWt(h), 0.0, [("KW", h, t) for t in range(NT)])
            for s_ in QSLOT:
                mset(Y[:, s_, :], 0.0, [("Y", s_, t) for t in range(NT)])
            mset(vs_[:, :, :, 64:65], 1.0, [("vs", t) for t in range(NT)])
            mset(vw_[:, :, :, 64:65], 1.0, [("vw", t) for t in range(NT)])
            if BIS == 1:
                return
            ssq = sm[:, 64:72]
            sd8 = sm[:, 72:80]
            r8 = sm[:, 80:88]
            gq = pp[:, l * NPP + C_GQ: l * NPP + C_GQ + 1]
            gks = pp[:, l * NPP + C_GK + 1: l * NPP + C_GK + 2]
            gkw = pp[:, l * NPP + C_GK + 2: l * NPP + C_GK + 3]
            qnb = [R(o_qn, 512, BF16), R(o_qn + 1024, 512, BF16)]

            ssqb = [sm[:, 64:72], sm[:, 240:248]]

            def stageA1(t):
                ba, bb = [0, 1, 6][t % 3], [2, 3, 7][t % 3]
                tc_ = slice(t * 128, (t + 1) * 128)
                ssq = ssqb[t % 2]
                mmg(ps(ba), [(XN[:, k, tc_], wbuf0[:, k, :]) for k in range(8)], [("XN", t), ("wbuf", 0)], [PK(ba)])
                mmg(ps(bb)[:, 0:268], [(XN[:, k, tc_], wbuf1[:, k, 0:268]) for k in range(8)], [("XN", t), ("wbuf", 1)], [PK(bb)])
                act(sq[:, 0:384], ps(ba)[:, 0:384], AF.Square, [PK(ba)], [("sq", 0)])
                act(sq[:, 384:512], ps(bb)[:, 0:128], AF.Square, [PK(bb)], [("sq", 1)])
                T.op("dve", lambda e, sq=sq, ssq=ssq: e.tensor_reduce(ssq, sq.rearrange("p (h d) -> p h d", h=8), AX.X, ALU.add),
                     [("sq", 0), ("sq", 1)], [("ssq", t % 2)])

            def stageA(t):
                ba, bb = [0, 1, 6][t % 3], [2, 3, 7][t % 3]
                qn = qnb[t % 2]
                qi = t % 2
                tc_ = slice(t * 128, (t + 1) * 128)
                ssq = ssqb[t % 2]
                act(sd8, ssq, AF.Sqrt, [("ssq", t % 2)], [("sd8",)], bias=1e-6, scale=1.0 / 64)
                recip(r8, sd8, [("sd8",)], [("r8",)])
                tt(qn[:, 0:256].rearrange("p (j i d) -> p i j d", j=2, i=2), ps(ba)[:, 0:256].rearrange("p (i j d) -> p i j d", i=2, j=2),
                   r8[:, 0:4].rearrange("p (i j) -> p i j", i=2).unsqueeze(3).broadcast_to([128, 2, 2, 64]), ALU.mult,
                   [PK(ba), ("r8",)], [("qn", qi, 0)])
                tt(qn[:, 256:384].rearrange("p (h d) -> p h d", h=2), ps(ba)[:, 256:384].rearrange("p (h d) -> p h d", h=2),
                   r8[:, 4:6].unsqueeze(2).broadcast_to([128, 2, 64]), ALU.mult, [PK(ba), ("r8",)], [("qn", qi, 2)])
                tt(qn[:, 384:512].rearrange("p (h d) -> p h d", h=2), ps(bb)[:, 0:128].rearrange("p (h d) -> p h d", h=2),
                   r8[:, 6:8].unsqueeze(2).broadcast_to([128, 2, 64]), ALU.mult, [PK(bb), ("r8",)], [("qn", qi, 1)])
                cpy(vs_[:, t, :, 0:64], ps(ba)[:, 384:512].rearrange("p (h d) -> p h d", h=2), [PK(ba)], [("vs", t)], eng="act")
                cpy(vw_[:, t, :, 0:64], ps(bb)[:, 128:256].rearrange("p (h d) -> p h d", h=2), [PK(bb)], [("vw", t)], eng="act")
                cpy(gs[:, t * 12:(t + 1) * 12], ps(bb)[:, 256:268], [PK(bb)], [("gs",)], eng="act")

            def stageB(t):
                bt = [4, 5][t % 2]
                qn = qnb[t % 2]
                qi = t % 2
                tc_ = slice(t * 128, (t + 1) * 128)
                pT = psb(bt).rearrange("p (a s) -> p a s", a=8)
                tps([(pT[:, 0, :], qn[:, 0:128]), (pT[:, 1, :], qn[:, 128:256]),
                     (pT[:, 2, :], qn[:, 256:384]), (pT[:, 3, :], qn[:, 384:512])], ident,
                    [("qn", qi, 0), ("qn", qi, 1), ("qn", qi, 2), ("cb",)], [PK(bt)])
                act(Y[0:64, QSLOT[0], tc_], pT[0:64, 0, :], AF.Copy, [PK(bt), ("pp",)], [("Y", QSLOT[0], t)], scale=gq[0:64, :])
                act(Y[64:128, QSLOT[2], tc_], pT[64:128, 0, :], AF.Copy, [PK(bt), ("pp",)], [("Y", QSLOT[2], t)], scale=gq[64:128, :])
                act(Y[0:64, QSLOT[1], tc_], pT[0:64, 1, :], AF.Copy, [PK(bt), ("pp",)], [("Y", QSLOT[1], t)], scale=gq[0:64, :])
                act(Y[64:128, QSLOT[3], tc_], pT[64:128, 1, :], AF.Copy, [PK(bt), ("pp",)], [("Y", QSLOT[3], t)], scale=gq[64:128, :])
                ts(Y[0:64, 6, tc_], pT[0:64, 2, :], gks[0:64, :], None, ALU.mult, None, [PK(bt), ("pp",)], [("Y", 6, t)])
                ts(Y[64:128, 7, tc_], pT[64:128, 2, :], gks[64:128, :], None, ALU.mult, None, [PK(bt), ("pp",)], [("Y", 7, t)])
                ts(KWt(0)[0:64, tc_], pT[0:64, 3, :], gkw[0:64, :], None, ALU.mult, None, [PK(bt), ("pp",)], [("KW", 0, t)])
                ts(KWt(1)[64:128, tc_], pT[64:128, 3, :], gkw[64:128, :], None, ALU.mult, None, [PK(bt), ("pp",)], [("KW", 1, t)])

            stageA1(0)
            stageA1(1)
            stageA(0)
            for t in range(NT):
                if t + 2 < NT:
                    stageA1(t + 2)
                if t + 1 < NT:
                    stageA(t + 1)
                stageB(t)
            wload(wbuf0, win_d[l, :, :, W_A:W_A + 512], [("wbuf", 0)])
            wload(wbuf1, win_d[l, :, :, W_C:W_C + 512], [("wbuf", 1)])
            act(gs, gs, AF.Sigmoid, [("gs",)], [("gs",)])

        def nsa_attn(l):
            vs_, vw_ = vaug(o_vs), vaug(o_vw)
            gs = R(o_gs, 192, F32).rearrange("p (t g) -> p t g", t=NT)
            yb = R(o_yb, 1024, F32).rearrange("p (q c) -> p q c", q=4)
            ybb = R(o_sq, 256, BF16)
            tmp = R(o_sq + 1024, 256, F32).rearrange("p (q c) -> p q c", q=4)
            rs4 = sm[:, 96:100]
            ri4 = sm[:, 100:104]
            cf4 = sm[:, 104:108]
            top8 = sm[:, 112:120]
            sc = sm[:, 128:160]
            def comb_T(c, qt, ybbs):
                t = 4 * c + qt
                bk = [6, 7][qt % 2]
                yq = ybbs[qt % 2]
                pT = psb(bk).rearrange("p (a s) -> p a s", a=8)
                tps([(pT[:, 0, :], yq[:, 0:128]), (pT[:, 1, :], yq[:, 128:256])], ident, [("ybb", qt % 2), ("cb",)], [PK(bk)])
                cpy(Y[:, 2:4, t * 128:(t + 1) * 128], pT[:, 0:2, :], [PK(bk)], [("Y", 2, t), ("Y", 3, t)], eng="dve")

            for c in range(4):
                ch = slice(c * 512, (c + 1) * 512)
                Es = []
                for hq in range(4):
                    h = hq // 2
                    bs_ = rot("aS", [0, 1])
                    Ei = hq
                    E = R(o_E[Ei], 512, BF16)
                    mm1(ps(bs_)[0:127, :], KC[:, h, 0:127], Y[:, QSLOT[hq], ch], True, True,
                        [("KC", h)] + yk(QSLOT[hq], c), [PK(bs_)])
                    act(E[0:127, :], ps(bs_)[0:127, :], AF.Exp, [PK(bs_)], [("E", Ei)], scale=0.125)
                    tt(E[0:127, :], E[0:127, :], cmpmask[0:127, ch], ALU.mult, [("E", Ei), ("cb",)], [("E", Ei)])
                    Es.append(E)
                for qp in range(2):
                    qts = [2 * qp, 2 * qp + 1]
                    pcl = []
                    for qi, qt in enumerate(qts):
                        bc = [4, 5][qi]
                        pc = ps(bc)[:, 0:388].rearrange("p (q c) -> p q c", q=4)
                        for hq in range(4):
                            mm1(pc[:, hq, :], Es[hq][0:127, qt * 128:(qt + 1) * 128], Rh[0:127, hq // 2, :], True, True,
                                [("E", hq), ("Rh", hq // 2)], [PK(bc)])
                        pcl.append((bc, pc))
                    for qi, qt in enumerate(qts):
                        bc, pc = pcl[qi]
                        ts(c_rs[qi], pc[:, :, 64], 1e-30, None, ALU.max, None, [PK(bc)], [("crs", qi)])
                    for qi, qt in enumerate(qts):
                        recip(c_ri[qi], c_rs[qi], [("crs", qi)], [("cri", qi)])
                    for h in range(2):
                        for qi, qt in enumerate(qts):
                            bc, pc = pcl[qi]
                            stt(c_sc[qi][h], pc[:, 2 * h, 65:97], c_ri[qi][:, 2 * h:2 * h + 1], fb[:, 4 * c + qt, :], ALU.mult, ALU.add,
                                [PK(bc), ("cri", qi), ("fb",)], [("csc", qi, h)])
                    for h in range(2):
                        for qi, qt in enumerate(qts):
                            bc, pc = pcl[qi]
                            stt(c_sc[qi][h], pc[:, 2 * h + 1, 65:97], c_ri[qi][:, 2 * h + 1:2 * h + 2], c_sc[qi][h], ALU.mult, ALU.add,
                                [PK(bc), ("cri", qi), ("csc", qi, h)], [("csc", qi, h)])
                    for h in range(2):
                        for qi, qt in enumerate(qts):
                            T.op("dve", lambda e, o=c_t8[qi][h], i_=c_sc[qi][h]: e.max(o, i_), [("csc", qi, h)], [("ct8", qi, h)])
                    for h in range(2):
                        col = 64 if h == 0 else 0
                        for qi, qt in enumerate(qts):
                            ts(c_nb[qi][:, col:col + 32], c_sc[qi][h], c_t8[qi][h][:, 7:8], NEGB, ALU.is_lt, ALU.mult,
                               [("csc", qi, h), ("ct8", qi, h)], [("cnb", qi)])
                    for qi, qt in enumerate(qts):
                        tt(c_cf[qi], gs[:, 4 * c + qt, 0:12:3], c_ri[qi], ALU.mult, [("gs",), ("cri", qi)], [("ccf", qi)])
                    for qi, qt in enumerate(qts):
                        bc, pc = pcl[qi]
                        tt(yb[:, qt, :].rearrange("p (h d) -> p h d", h=4), pc[:, :, 0:64],
                           c_cf[qi].unsqueeze(2).broadcast_to([128, 4, 64]), ALU.mult, [PK(bc), ("ccf", qi)], [("yb", qt)])
                    for qi, qt in enumerate(qts):
                        t = 4 * c + qt
                        tc_ = slice(t * 128, (t + 1) * 128)
                        bT = [6, 7][qi]
                        pT = psb(bT)
                        tps([(pT[:, 0:128], c_nb[qi])], ident, [("cnb", qi), ("cb",)], [PK(bT)])
                        cpy(Y[0:32, QSLOT[2], tc_], pT[0:32, 0:128], [PK(bT)], [("Y", QSLOT[2], t)], eng="act")
                        cpy(Y[0:32, QSLOT[3], tc_], pT[0:32, 0:128], [PK(bT)], [("Y", QSLOT[3], t)], eng="act")
                        cpy(Y[64:96, QSLOT[0], tc_], pT[64:96, 0:128], [PK(bT)], [("Y", QSLOT[0], t)], eng="dve")
                        cpy(Y[64:96, QSLOT[1], tc_], pT[64:96, 0:128], [PK(bT)], [("Y", QSLOT[1], t)], eng="dve")
                items = []
                for br in range(2):
                    for hq in range(4):
                        bo = rot("aO", [2, 3])
                        kts = list(range(0, 4 * c + 4)) if br == 0 else list(range(max(0, 4 * c - 4), 4 * c + 4))
                        for ix, kt in enumerate(kts):
                            n_ = len(items)
                            items.append(dict(br=br, hq=hq, h=hq // 2, bo=bo, kt=kt, first=(ix == 0), lastg=(ix == len(kts) - 1),
                                              bs=[0, 1, 7, 6, 4, 5][n_ % 6], Ei=n_ % 8))

                def geom(it):
                    kt, br = it["kt"], it["br"]
                    qlo = max(kt, 4 * c)
                    qhi = 4 * c + 3 if br == 0 else min(kt + 4, 4 * c + 3)
                    return qlo, qhi, (qhi - qlo + 1) * 128

                def emit_S(it):
                    qlo, qhi, N = geom(it)
                    kt, br, hq, h, bs_ = it["kt"], it["br"], it["hq"], it["h"], it["bs"]
                    qc = slice(qlo * 128, (qhi + 1) * 128)
                    kc_ = slice(kt * 128, (kt + 1) * 128)
                    qkeys = [("Y", QSLOT[hq], t_) for t_ in range(qlo, qhi + 1)]
                    if br == 0:
                        mm1(ps(bs_)[:, 0:N], Y[:, 6 + h, kc_], Y[:, QSLOT[hq], qc], True, True,
                            [("Y", 6 + h, kt)] + qkeys, [PK(bs_)])
                    else:
                        mm1(ps(bs_)[:, 0:N], KWt(h)[:, kc_], Y[:, QSLOT[hq], qc], True, True,
                            [("KW", h, kt)] + qkeys, [PK(bs_)])

                def emit_exp(it):
                    qlo, qhi, N = geom(it)
                    kt, br, bs_, Ei = it["kt"], it["br"], it["bs"], it["Ei"]
                    E = R(o_E[Ei], 512, BF16)
                    act(E[:, 0:N], ps(bs_)[:, 0:N], AF.Exp, [PK(bs_)], [("E", Ei)], scale=0.125)
                    if kt >= 4 * c:
                        tt(E[:, 0:128], E[:, 0:128], M_le, ALU.mult, [("E", Ei), ("cb",)], [("E", Ei)])
                    if br == 1 and kt + 4 <= 4 * c + 3:
                        tt(E[:, N - 128:N], E[:, N - 128:N], M_gt, ALU.mult, [("E", Ei), ("cb",)], [("E", Ei)])

                def emit_PV(it):
                    qlo, qhi, N = geom(it)
                    kt, br, hq, h, bo, Ei = it["kt"], it["br"], it["hq"], it["h"], it["bo"], it["Ei"]
                    E = R(o_E[Ei], 512, BF16)
                    po = ps(bo)[:, 0:260].rearrange("p (q c) -> p q c", q=4)
                    va = vs_ if br == 0 else vw_
                    vkey = ("vs", kt) if br == 0 else ("vw", kt)
                    first = it["first"]
                    for qt_ in range(qlo, qhi + 1):
                        lastmm = (it["lastg"] and qt_ == qhi)
                        mm1(po[:, qt_ - 4 * c, :], E[:, (qt_ - qlo) * 128:(qt_ - qlo + 1) * 128], va[:, kt, h, :],
                            first, lastmm, [("E", Ei), vkey], [PK(bo)], skip=True)
                        first = False
                    if it["lastg"]:
                        deferred.append([3, lambda po=po, bo=bo, hq=hq, br=br: evac(po, bo, hq, br)])

                def evac(po, bo, hq, br):
                    if True:
                        recip(ri4, po[:, :, 64], [PK(bo)], [("ri4",)])
                        gcol = hq * 3 + 1 + br
                        tt(cf4, gs[:, 4 * c:4 * c + 4, gcol], ri4, ALU.mult, [("gs",), ("ri4",)], [("cf4",)])
                        tt(tmp, po[:, :, 0:64], cf4.unsqueeze(2).broadcast_to([128, 4, 64]), ALU.mult, [PK(bo), ("cf4",)], [("tmpa",)])
                        tt(yb[:, :, hq * 64:(hq + 1) * 64], yb[:, :, hq * 64:(hq + 1) * 64], tmp, ALU.add,
                           [("tmpa",)] + [("yb", q) for q in range(4)], [("yb", q) for q in range(4)])

                LOOK = 5
                deferred = []
                for i in range(min(LOOK, len(items))):
                    emit_S(items[i])
                for i in range(len(items)):
                    emit_exp(items[i])
                    if i + LOOK < len(items):
                        emit_S(items[i + LOOK])
                    for d_ in deferred:
                        d_[0] -= 1
                    while deferred and deferred[0][0] <= 0:
                        deferred.pop(0)[1]()
                    emit_PV(items[i])
                while deferred:
                    deferred.pop(0)[1]()
                ybbs = [ybb, R(o_sq + 512, 256, BF16)]
                for qt in range(4):
                    cpy(ybbs[qt % 2], yb[:, qt, :], [("yb", qt)], [("ybb", qt % 2)], eng="act")
                    if qt >= 1:
                        comb_T(c, qt - 1, ybbs)
                comb_T(c, 3, ybbs)

        def gmlp(l):
            T.barrier()
            wbuf0 = R(0, 4096, BF16).rearrange("p (k c) -> p k c", k=8)
            u = R(16384, NT * 256, BF16).rearrange("p (t c) -> p t c", t=NT)
            v = R(16384 + 8192, NT * 256, F32).rearrange("p (t c) -> p t c", t=NT)
            o2 = 16384 + 8192 + 16384
            wsf = R(o2, 512, F32)
            wsm = R(o2 + 2048, 512, BF16)
            vn = R(o2 + 3072, 256, F32)
            vb = R(o2 + 4096, 256, BF16)
            ya = R(o2 + 4608, 256, BF16)
            st6 = sm[:, 160:166]
            mv = sm[:, 168:200].rearrange("p (t a) -> p t a", t=NT)
            sd16 = sm[:, 200:216]
            rs16 = sm[:, 216:232]
            dma("sp", wsf, ws_d[l, :, :], (), [("wsf",)])
            tt(wsm.rearrange("p (g s) -> p g s", g=4), wsf.rearrange("p (g s) -> p g s", g=4),
               M_ge.unsqueeze(1).broadcast_to([128, 4, 128]), ALU.mult, [("wsf",), ("cb",)], [("wsm",)])
            pT = psb(6).rearrange("p (a s) -> p a s", a=8)
            tps([(pT[:, g, :], wsm[:, g * 128:(g + 1) * 128]) for g in range(4)], ident, [("wsm",), ("cb",)], [PK(6)])
            cpy(wsT, pT[:, 0:4, :], [PK(6)], [("wsT",)], eng="dve")
            for t in range(NT):
                ba = rot("ga", [0, 1])
                tc_ = slice(t * 128, (t + 1) * 128)
                mmg(ps(ba), [(XN[:, k, tc_], wbuf0[:, k, :]) for k in range(8)], [("XN", t), ("wbuf", 0)], [PK(ba)])
                act(u[:, t, :], ps(ba)[:, 0:256], AF.Gelu_apprx_tanh, [PK(ba)], [("u", t)])
                act(v[:, t, :], ps(ba)[:, 256:512], AF.Gelu_apprx_tanh, [PK(ba)], [("v", t)])
                T.op("dve", lambda e, t=t: e.bn_stats(st6, v[:, t, :]), [("v", t)], [("st6",)])
                T.op("dve", lambda e, t=t: e.bn_aggr(mv[:, t, :], st6), [("st6",)], [("mv", t)])
            wload(wbuf0, win_d[l, :, :, W_D:W_D + 512], [("wbuf", 0)])
            act(sd16, mv[:, :, 1], AF.Sqrt, [("mv", t) for t in range(NT)], [("sd16",)], bias=1e-5, scale=1.0)
            recip(rs16, sd16, [("sd16",)], [("rs16",)])
            lng = pp[:, l * NPP + C_LNG: l * NPP + C_LNG + 256]
            lnb = pp[:, l * NPP + C_LNB: l * NPP + C_LNB + 256]
            bsT = pp[:, l * NPP + C_BS: l * NPP + C_BS + 4]
            vbb = [vb, R(o2 + 5120, 256, BF16)]
            yab = [ya, R(o2 + 5632, 256, BF16)]

            def g1(t):
                vb_ = vbb[t % 2]
                bm = [2, 3][t % 2]
                stt(vn, v[:, t, :], mv[:, t, 0:1], lng, ALU.subtract, ALU.mult, [("v", t), ("mv", t), ("pp",)], [("vn",)])
                stt(vb_, vn, rs16[:, t:t + 1], lnb, ALU.mult, ALU.add, [("vn",), ("rs16",), ("pp",)], [("vb", t % 2)])
                for g in range(4):
                    mm1(ps(bm)[:, g * 64:(g + 1) * 64], wsT[:, g, :], vb_[:, g * 64:(g + 1) * 64], True, True,
                        [("wsT",), ("vb", t % 2)], [PK(bm)])

            def g2(t):
                bm, bt = [2, 3][t % 2], [4, 5][t % 2]
                ya_ = yab[t % 2]
                for g in range(4):
                    stt(ya_[:, g * 64:(g + 1) * 64], ps(bm)[:, g * 64:(g + 1) * 64], bsT[:, g:g + 1], u[:, t, g * 64:(g + 1) * 64],
                        ALU.add, ALU.mult, [PK(bm), ("pp",), ("u", t)], [("ya", t % 2)])
                pT = psb(bt).rearrange("p (a s) -> p a s", a=8)
                tps([(pT[:, 0, :], ya_[:, 0:128]), (pT[:, 1, :], ya_[:, 128:256])], ident, [("ya", t % 2), ("cb",)], [PK(bt)])
                cpy(Y[:, 0:2, t * 128:(t + 1) * 128], pT[:, 0:2, :], [PK(bt)], [("Y", 0, t), ("Y", 1, t)], eng="act")

            g1(0)
            for t in range(NT):
                if t + 1 < NT:
                    g1(t + 1)
                g2(t)

        def conf(l):
            T.barrier()
            wbuf0 = R(8192, 4096, BF16).rearrange("p (k c) -> p k c", k=8)
            ZW = 30 + S
            zpad = R(16384, 2 * ZW, BF16).rearrange("p (c s) -> p c s", c=2)
            o2 = 16384 + 8320
            dg = R(o2, 2 * 31 * 128, BF16).rearrange("p (c j e) -> p c j e", c=2, j=31)
            o3 = o2 + 15872
            acc = R(o3, 1024, F32).rearrange("p (c s) -> p c s", c=2)
            sqc = R(o3 + 4096, 1024, F32).rearrange("p (c s) -> p c s", c=2)
            sg = R(o3 + 8192, 512, F32)
            msq = R(o3 + 10240, 512, F32)
            var = R(o3 + 12288, 512, F32)
            rstd = R(o3 + 14336, 512, F32)
            zc = R(o3 + 16384, 512, F32)
            base = l * NPP
            mset(zpad[:, :, 0:30], 0.0, [("zpad", -1)])
            for cc in range(2):
                cw = pp[:, base + C_CW + cc * 31: base + C_CW + (cc + 1) * 31]
                tt(dg[:, cc, :, :], ident.unsqueeze(1).broadcast_to([128, 31, 128]), cw.unsqueeze(2).broadcast_to([128, 31, 128]),
                   ALU.mult, [("cb",), ("pp",)], [("dg", cc)])
            def cproj(c):
                ch = slice(c * 512, (c + 1) * 512)
                for cc in range(2):
                    ba, bb = cc, 2 + cc
                    mmg(ps(ba), [(wbuf0[:, k, cc * 128:(cc + 1) * 128], XN[:, k, ch]) for k in range(8)], [("wbuf", 1)] + xnk(c), [PK(ba)])
                    mmg(ps(bb), [(wbuf0[:, k, 256 + cc * 128:256 + (cc + 1) * 128], XN[:, k, ch]) for k in range(8)],
                        [("wbuf", 1)] + xnk(c), [PK(bb)])

            sgs = [sg, R(o3 + 18432, 512, F32)]
            cproj(0)
            for c in range(4):
                ch = slice(c * 512, (c + 1) * 512)
                for cc in range(2):
                    ba, bb = cc, 2 + cc
                    act(sgs[cc], ps(bb), AF.Sigmoid, [PK(bb)], [("sg", cc)])
                    tt(zpad[:, cc, 30 + c * 512: 30 + (c + 1) * 512], ps(ba), sgs[cc], ALU.mult, [PK(ba), ("sg", cc)], [("zpad", c, cc)])
                for cc in range(2):
                    bcv = [6, 7][cc]
                    mmg(ps(bcv), [(dg[:, cc, j, :], zpad[:, cc, c * 512 + j: c * 512 + j + 512]) for j in range(31)],
                        [("dg", cc), ("zpad", c, cc), ("zpad", c - 1, cc), ("zpad", -1)], [PK(bcv)])
                for cc in range(2):
                    bcv = [6, 7][cc]
                    cbias = pp[:, base + C_CB + cc: base + C_CB + cc + 1]
                    act(acc[:, cc, :], ps(bcv), AF.Identity, [PK(bcv), ("pp",)], [("acc", cc)], bias=cbias, scale=1.0)
                    act(sqc[:, cc, :], acc[:, cc, :], AF.Square, [("acc", cc)], [("sqc", cc)])
                mmg(ps(4), [(ones256, acc[:, cc, :]) for cc in range(2)], [("ones",), ("acc", 0), ("acc", 1)], [PK(4)])
                mmg(ps(5), [(ones256, sqc[:, cc, :]) for cc in range(2)], [("ones",), ("sqc", 0), ("sqc", 1)], [PK(5)])
                if c < 3:
                    cproj(c + 1)
                act(msq, ps(4), AF.Square, [PK(4)], [("msq",)])
                tt(var, ps(5), msq, ALU.subtract, [PK(5), ("msq",)], [("var",)])
                act(var, var, AF.Sqrt, [("var",)], [("var",)], bias=1e-5, scale=1.0)
                recip(rstd, var, [("var",)], [("rstdc",)])
                for cc in range(2):
                    tt(zc, acc[:, cc, :], ps(4), ALU.subtract, [("acc", cc), PK(4)], [("zc",)])
                    tt(zc, zc, rstd, ALU.mult, [("zc",), ("rstdc",)], [("zc",)])
                    act(Y[:, 4 + cc, ch], zc, AF.Silu, [("zc",), ("pp",)], yk(4 + cc, c),
                        bias=pp[:, base + C_CLB + cc: base + C_CLB + cc + 1], scale=pp[:, base + C_CLG + cc: base + C_CLG + cc + 1])
            wload(wbuf0[:, :, 0:256], win_d[l, :, :, W_D + 512:W_D + 768], [("wbuf", 1)])

        def sconv(l):
            T.barrier()
            wbuf0 = R(0, 4096, BF16).rearrange("p (k c) -> p k c", k=8)
            wbuf1 = R(8192, 4096, BF16).rearrange("p (k c) -> p k c", k=8)
            MW = 2 + S
            mpad = R(16384, 2 * MW, F32).rearrange("p (c s) -> p c s", c=2)
            o2 = 16384 + 2 * MW * 4
            shs = R(o2, 512, F32)
            acc = R(o2 + 2048, 512, F32)
            base = l * NPP
            mset(mpad[:, :, 0:2], 0.0, [("mpad", -1)])
            for c in range(4):
                ch = slice(c * 512, (c + 1) * 512)
                for cc in range(2):
                    bB, bC, bH = rot("dB", [0, 1]), rot("dC", [2, 3]), rot("dH", [4, 5])
                    cs_ = slice(cc * 128, (cc + 1) * 128)
                    mmg(ps(bB), [(wbuf0[:, k, cs_], XN[:, k, ch]) for k in range(8)], [("wbuf", 0)] + xnk(c), [PK(bB)])
                    mmg(ps(bC), [(wbuf0[:, k, 256 + cc * 128:256 + (cc + 1) * 128], XN[:, k, ch]) for k in range(8)],
                        [("wbuf", 0)] + xnk(c), [PK(bC)])
                    mmg(ps(bH), [(wbuf1[:, k, cs_], XN[:, k, ch]) for k in range(8)], [("wbuf", 1)] + xnk(c), [PK(bH)])
                    cpy(shs, ps(bH), [PK(bH)], [("shs",)], eng="act")
                    tt(mpad[:, cc, 2 + c * 512: 2 + (c + 1) * 512], ps(bC), shs, ALU.mult, [PK(bC), ("shs",)], [("mpad", c, cc)])
                    sw = pp[:, base + C_SW + cc * 3: base + C_SW + cc * 3 + 3]
                    mr = [("mpad", c, cc), ("mpad", c - 1, cc), ("mpad", -1), ("pp",)]
                    ts(acc, mpad[:, cc, c * 512: c * 512 + 512], sw[:, 0:1], None, ALU.mult, None, mr, [("accd",)])
                    for j in (1, 2):
                        stt(acc, mpad[:, cc, c * 512 + j: c * 512 + j + 512], sw[:, j:j + 1], acc, ALU.mult, ALU.add,
                            mr + [("accd",)], [("accd",)])
                    tt(Y[:, 6 + cc, ch], ps(bB), acc, ALU.mult, [PK(bB), ("accd",)], yk(6 + cc, c))

        def gate_phase(l):
            T.barrier()
            mixT = R(0, 8 * S, BF16).rearrange("p (j s) -> p j s", j=8)
            wgb = [R(32768 + 8192 * i, 4096, BF16).rearrange("p (k b e) -> p k b e", k=8, b=4) for i in range(2)]
            wbb = [R(49152 + 2048 * i, 1024, BF16).rearrange("p (a e) -> p a e", a=8) for i in range(2)]
            gsb = [R(53248 + 2048 * i, 512, F32) for i in range(2)]
            prod = [R(57344 + 1024 * i, 512, BF16) for i in range(4)]
            base = l * NPP
            gpend = []
            for j in range(8):
                wi = j % 2
                wload_flat(32768 + 8192 * wi, 4096, wg_d[l, j, :, :], [("wg", wi)])
                wload(wbb[wi], wb_d[l, j, :, :].rearrange("p (a e) -> p a e", a=8), [("wb", wi)])
                for c in range(4):
                    ch = slice(c * 512, (c + 1) * 512)
                    for b in range(4):
                        bg, bp = rot("gG", [0, 1]), rot("gP", [2, 3])
                        gi = rot("gsb", [0, 1])
                        mmg(ps(bg), [(wgb[wi][:, k, b, :], XN[:, k, ch]) for k in range(8)], [("wg", wi)] + xnk(c), [PK(bg)])
                        act(gsb[gi], ps(bg), AF.Sigmoid, [PK(bg), ("pp",)], [("gsb", gi)],
                            bias=pp[:, base + C_BG + b * 8 + j: base + C_BG + b * 8 + j + 1], scale=1.0)
                        mmg(ps(bp), [(wbb[wi][:, b * 2 + cc, :], Y[:, b * 2 + cc, ch]) for cc in range(2)],
                            [("wb", wi)] + yk(b * 2, c) + yk(b * 2 + 1, c), [PK(bp)])
                        if b == 0 and gpend:
                            gpend.pop(0)()
                        tt(prod[b], ps(bp), gsb[gi], ALU.mult, [PK(bp), ("gsb", gi)], [("prod", b)])

                    def fin(j=j, c=c, ch=ch):
                        bm = rot("gM", [4, 5])
                        mmg(ps(bm), [(ident, prod[b]) for b in range(4)], [("cb",)] + [("prod", b) for b in range(4)], [PK(bm)])
                        cpy(mixT[:, j, ch], ps(bm), [PK(bm)], [("mix", j, c)], eng="act")
                    gpend.append(fin)
            while gpend:
                gpend.pop(0)()
            wo = R(32768, 8192, BF16).rearrange("p (j e) -> p j e", j=8)
            wload_flat(32768, 8192, wo_d[l, :, :, :].rearrange("p j e -> p (j e)"), [("wo",), ("wg", 0), ("wg", 1)])
            for t in range(NT):
                for hf in range(2):
                    bo = rot("gO", [6, 7, 0, 1])
                    mmg(ps(bo), [(mixT[:, j, t * 128:(t + 1) * 128], wo[:, j, hf * 512:(hf + 1) * 512]) for j in range(8)],
                        [("wo",)] + [("mix", j, t // 4) for j in range(8)], [PK(bo)])
                    tt(X[:, t, hf * 512:(hf + 1) * 512], X[:, t, hf * 512:(hf + 1) * 512], ps(bo), ALU.add,
                       [("X", t), PK(bo)], [("X", t)])

        def ffn(l, last):
            hid = Y
            w1b = R(0, 8192, BF16).rearrange("p (k f) -> p k f", k=8)
            w2b = R(16384, 8192, BF16).rearrange("p (k f) -> p k f", k=8)
            rt = [R(32768 + 2048 * i, 512, F32) for i in range(2)]
            for fblk in range(4):
                al1 = [("mix", j, c) for j in range(0, 4) for c in range(4)] if fblk == 0 else []
                al2 = [("mix", j, c) for j in range(4, 8) for c in range(4)] if fblk == 0 else []
                wload_flat(0, 8192, w1_d[l, fblk, :, :, :].rearrange("p k f -> p (k f)"), [("w1b",)] + al1)
                wload_flat(16384, 8192, w2_d[l, fblk, :, :, :].rearrange("p k f -> p (k f)"), [("w2b",)] + al2)
                for fc in range(8):
                    for c in range(4):
                        ch = slice(c * 512, (c + 1) * 512)
                        bh = rot("fH", [0, 1, 2, 3])
                        ri = rot("frt", [0, 1])
                        mmg(ps(bh), [(w1b[:, k, fc * 128:(fc + 1) * 128], XN[:, k, ch]) for k in range(8)], [("w1b",)] + xnk(c), [PK(bh)])
                        act(rt[ri], ps(bh), AF.Relu, [PK(bh)], [("rt", ri)] + ([("wo",)] if (fblk == 0 and fc == 0 and c < 2) else []))
                        tt(hid[:, fc, ch], rt[ri], rt[ri], ALU.mult, [("rt", ri)], yk(fc, c))
                for t in range(NT):
                    for hf in range(2):
                        bo = rot("fO", [4, 5, 6, 7])
                        mmg(ps(bo), [(hid[:, fc, t * 128:(t + 1) * 128], w2b[:, fc, hf * 512:(hf + 1) * 512]) for fc in range(8)],
                            [("w2b",)] + [("Y", fc, t) for fc in range(8)], [PK(bo)])
                        tt(X[:, t, hf * 512:(hf + 1) * 512], X[:, t, hf * 512:(hf + 1) * 512], ps(bo), ALU.add,
                           [("X", t), PK(bo)], [("X", t)])
                    if last and fblk == 3:
                        dma("sp", out_d[t * 128:(t + 1) * 128, :], X[:, t, :], [("X", t)], [("out", t)])

        stopped = False
        for l in range(n_layers):
            phases = [("N", lambda: norm_phase(l, C_G1)), ("C", lambda: nsa_compress(l)), ("J", lambda: nsa_proj(l)),
                      ("T", lambda: nsa_attn(l)), ("A", lambda: gmlp(l)), ("V", lambda: conf(l)), ("P", lambda: sconv(l)),
                      ("G", lambda: gate_phase(l)), ("M", lambda: norm_phase(l, C_G2)),
                      ("F", lambda: ffn(l, (l == n_layers - 1) and debug_stop is None))]
            _wb0 = R(0, 4096, BF16).rearrange("p (k c) -> p k c", k=8)
            _wb1 = R(8192, 4096, BF16).rearrange("p (k c) -> p k c", k=8)
            wload(_wb0[:, :, 0:256], win_d[l, :, :, W_BF:W_BF + 256], [("wbuf", 0), ("w1b",)])
            wload(_wb1[:, :, 0:268], win_d[l, :, :, W_BT2:W_BT2 + 268], [("wbuf", 1), ("w1b",)])
            for nm, fnp in phases:
                fnp()
                if debug_stop == "%s%d" % (nm, l):
                    store_dbg(); stopped = True; break
            if stopped:
                break

        fin = T.final_tokens()

        with nc.Block() as block:
            def run(engname, e):
                for waits, fn, inc, dsem in T.ops[engname]:
                    for sid, v in waits:
                        e.wait_ge(sems[sid], v)
                    if dsem is not None:
                        fn(e, sems[dsem])
                    else:
                        ins = fn(e)
                        ins.then_inc(sems[inc[0]], inc[1])
                if engname == "sp":
                    for sid, v in fin.items():
                        if v > 0:
                            e.wait_ge(sems[sid], v)

            @block.tensor
            def _(e):
                run("pe", e)

            @block.scalar
            def _(e):
                run("act", e)

            @block.vector
            def _(e):
                run("dve", e)

            @block.gpsimd
            def _(e):
                run("pool", e)

            @block.sync
            def _(e):
                run("sp", e)
    return nc


def _overlap_matrix():
    cs = np.arange(127) * 16
    ce = cs + 32
    ss = np.arange(32) * 64
    se = ss + 64
    return ((cs[:, None] < se[None, :]) & (ce[:, None] > ss[None, :])).astype(np.float32)


def _host_inputs(inp):
    f = lambda a: np.ascontiguousarray(np.asarray(a, dtype=np.float32))
    bf = lambda a: np.ascontiguousarray(np.asarray(a, dtype=np.float32).astype(ml_dtypes.bfloat16))
    pp = np.zeros((128, L, NPP), np.float32)
    for l in range(L):
        pp[:, l, C_G1:C_G1 + 8] = f(inp["norm1_g"])[l].reshape(8, 128).T
        pp[:, l, C_G2:C_G2 + 8] = f(inp["norm2_g"])[l].reshape(8, 128).T
        pp[:, l, C_BS:C_BS + 4] = f(inp["gmlp_bs"])[l].T
        pp[:, l, C_GQ] = np.tile(f(inp["nsa_q_norm_g"])[l], 2)
        for i in range(3):
            pp[:, l, C_GK + i] = np.tile(f(inp["nsa_k_norm_g"])[l, i], 2)
        cw = f(inp["conf_conv_w"])[l]
        for cc in range(2):
            pp[:, l, C_CW + cc * 31:C_CW + (cc + 1) * 31] = cw[:, cc * 128:(cc + 1) * 128].T
        pp[:, l, C_CB:C_CB + 2] = f(inp["conf_conv_b"])[l].reshape(2, 128).T
        pp[:, l, C_CLG:C_CLG + 2] = f(inp["conf_ln_g"])[l].reshape(2, 128).T
        pp[:, l, C_CLB:C_CLB + 2] = f(inp["conf_ln_b"])[l].reshape(2, 128).T
        sw = f(inp["sconv_w"])[l]
        for cc in range(2):
            pp[:, l, C_SW + cc * 3:C_SW + (cc + 1) * 3] = sw[:, cc * 128:(cc + 1) * 128].T
        pp[:, l, C_BG:C_BG + 32] = f(inp["b_gate"])[l].reshape(32, 128).T
        pe = f(inp["nsa_cmp_pe"])[l]
        for kv in range(2):
            pp[:, l, C_PE + kv * 32:C_PE + (kv + 1) * 32] = np.tile(pe[kv].T, (2, 1))
        pp[:, l, C_LNG:C_LNG + 256] = np.broadcast_to(f(inp["gmlp_ln_g"])[l][None, :], (128, 256))
        pp[:, l, C_LNB:C_LNB + 256] = np.broadcast_to(f(inp["gmlp_ln_b"])[l][None, :], (128, 256))
    pp = pp.reshape(128, L * NPP)
    p = np.arange(128)[:, None]
    q = np.arange(128)[None, :]
    cb = np.zeros((128, 2560), np.float32)
    cb[:, 0:128] = (p == q)
    cb[:, 128:256] = (p <= q)
    cb[:, 256:384] = (p >= q)
    cb[:, 384:512] = (p > q)
    n = np.arange(128)[:, None]
    s = np.arange(S)[None, :]
    cb[:, 512:2560] = ((16 * n + 31) <= s) & (n < 127)
    eall = np.zeros((2, 128, S), np.float32)
    E = (np.arange(32)[:, None] == (np.arange(S)[None, :] // 64)).astype(np.float32)
    eall[0, 64:96, :] = E
    eall[1, 0:32, :] = E
    rconst = np.zeros((127, 33), np.float32)
    rconst[:, 0] = 1.0
    rconst[:, 1:33] = _overlap_matrix()
    blk = np.arange(32)[None, :]
    cur = (np.arange(S) // 64)[:, None]
    forced = (blk == 0) | ((cur - blk >= 0) & (cur - blk < 2))
    causal = blk <= cur
    fbias = np.where(forced, 100.0, np.where(causal, 0.0, -100.0)).astype(np.float32)
    fbias = fbias.reshape(NT, 128, 32).transpose(1, 0, 2).reshape(128, NT * 32)
    w_in = f(inp["w_in"])
    perm = np.concatenate([np.arange(0, 512), np.arange(512, 768), np.arange(1024, 1280), np.arange(1280, 1548),
                           np.arange(768, 1024), np.arange(1548, 2828)])
    win = w_in[:, :, perm].reshape(L, 8, 128, 2828).transpose(0, 2, 1, 3)
    wgate = f(inp["w_gate"]).reshape(L, 8, 128, 4, 8, 128)
    wg = wgate.transpose(0, 4, 2, 1, 3, 5).reshape(L, 8, 128, 8 * 4 * 128)
    wbr = f(inp["w_branch"]).reshape(L, 4, 2, 128, 8, 128)
    wb = wbr.transpose(0, 4, 3, 1, 2, 5).reshape(L, 8, 128, 8 * 128)
    wo = f(inp["w_out"]).reshape(L, 8, 128, 1024).transpose(0, 2, 1, 3)
    w1 = f(inp["w_mlp1"]).reshape(L, 8, 128, 4, 1024).transpose(0, 3, 2, 1, 4)
    w2 = f(inp["w_mlp2"]).reshape(L, 4, 8, 128, 1024).transpose(0, 1, 3, 2, 4)
    cw1 = f(inp["nsa_cmp_w1"]).reshape(L, 2, 32, 64, 128).transpose(0, 1, 3, 2, 4)
    cw1 = np.concatenate([cw1, cw1], axis=2)
    cw2 = f(inp["nsa_cmp_w2"]).transpose(0, 2, 1, 3).reshape(L, 128, 128)
    ws = f(inp["gmlp_ws"]).transpose(0, 2, 1, 3).reshape(L, 128, 512)
    c = np.ascontiguousarray
    return dict(pp=c(pp), cb16=bf(cb), eall=bf(eall), rconst=bf(rconst), fbias=c(fbias), win=c(win), wg=c(wg), wb=c(wb),
                wo=c(wo), w1=c(w1), w2=c(w2), cw1=c(cw1), cw2=c(cw2), ws=c(ws))


_NC_CACHE = {}


def kernel(**inputs):
    shared = _host_inputs(inputs)
    x = np.ascontiguousarray(np.asarray(inputs["x"], dtype=np.float32))
    B = x.shape[0]
    key = (N_LAYERS, DEBUG_STOP)
    nc = build_nc(N_LAYERS, DEBUG_STOP)
    in_maps = []
    for b in range(B):
        m = dict(shared)
        m["x"] = x[b]
        in_maps.append(m)
    res = run_bass_kernel_spmd(nc, in_maps, core_ids=list(range(B)))
    out = np.stack([np.asarray(r["out"], dtype=np.float32) for r in res.results], axis=0)
    if DEBUG_STOP is not None:
        kernel.dbg = [np.asarray(r["dbgy"]) for r in res.results]
    return out
```

```python
import contextlib
import numpy as np
import ml_dtypes
import concourse.bass as bass
import concourse.mybir as mybir
from concourse.bass_utils import run_bass_kernel_spmd

F32 = mybir.dt.float32
BF16 = mybir.dt.bfloat16
AF = mybir.ActivationFunctionType
ALU = mybir.AluOpType
AX = mybir.AxisListType

L = 2
S = 2048
D = 1024
NT = 16
NPP = 706
NEGB = -30000.0
BIS = 0
DEBUG_STOP = None
N_LAYERS = L

C_G1, C_G2, C_BS, C_GQ, C_GK, C_CW, C_CB, C_CLG, C_CLB, C_SW, C_BG, C_PE, C_LNG, C_LNB = \
    0, 8, 16, 20, 21, 24, 86, 88, 90, 92, 98, 130, 194, 450
W_A, W_BT1, W_BT2, W_BF, W_C, W_D = 0, 512, 1024, 1292, 1548, 2060


class TK:
    ENGS = ("pe", "act", "dve", "pool", "sp")
    NDS = 16

    def __init__(self):
        self.cnt = {e: 0 for e in self.ENGS}
        self.ops = {e: [] for e in self.ENGS}
        self.lw = {}
        self.rd = {}
        self.waited = {e: {} for e in self.ENGS}
        self.dval = [0] * self.NDS
        self.dnext = 0
        self.dnext_g = 0
        self.pending = {e: {} for e in self.ENGS}

    def _collect(self, eng, reads, writes):
        deps = {}

        def add(tok):
            sid, v = tok
            if deps.get(sid, 0) < v:
                deps[sid] = v
        for k in reads:
            t = self.lw.get(k)
            if t is not None:
                if t[0] == eng and eng == "pe":
                    continue
                add(t)
        for k in writes:
            t = self.lw.get(k)
            if t is not None and not (t[0] == eng and eng == "pe"):
                add(t)
            for sid, v in self.rd.get(k, {}).items():
                if not (sid == eng and eng == "pe"):
                    add((sid, v))
        for sid, v in self.pending[eng].items():
            add((sid, v))
        self.pending[eng] = {}
        waits = []
        w = self.waited[eng]
        for sid, v in deps.items():
            if w.get(sid, 0) < v:
                w[sid] = v
                waits.append((sid, v))
        return waits

    def _commit(self, tok, reads, writes):
        sid, v = tok
        for k in reads:
            d = self.rd.setdefault(k, {})
            if d.get(sid, 0) < v:
                d[sid] = v
        for k in writes:
            self.lw[k] = tok
            self.rd[k] = {}

    def op(self, eng, fn, reads=(), writes=()):
        writes = list(writes) + [k for k in reads if k[0] == "ps"]
        reads = [k for k in reads if k[0] != "ps"]
        waits = self._collect(eng, reads, writes)
        self.cnt[eng] += 1
        tok = (eng, self.cnt[eng])
        self.ops[eng].append((waits, fn, (eng, 1), None))
        self._commit(tok, reads, writes)

    def dma(self, eng, fn, n, reads=(), writes=()):
        half = self.NDS // 2
        if eng == "pool":
            j = half + self.dnext_g
            self.dnext_g = (self.dnext_g + 1) % half
        else:
            j = self.dnext
            self.dnext = (self.dnext + 1) % half
        sid = "d%d" % j
        waits = self._collect(eng, reads, writes)
        v0 = self.dval[j]
        if self.waited[eng].get(sid, 0) < v0:
            self.waited[eng][sid] = v0
            waits.append((sid, v0))
        self.dval[j] = v0 + 16 * n
        tok = (sid, self.dval[j])
        self.ops[eng].append((waits, fn, None, sid))
        self._commit(tok, reads, writes)
        return tok

    def barrier(self):
        toks = {e: self.cnt[e] for e in ("pe", "act", "dve", "pool")}
        for j in range(self.NDS):
            toks["d%d" % j] = self.dval[j]
        for e in self.ENGS:
            for sid, v in toks.items():
                if v > 0 and sid != e:
                    if self.pending[e].get(sid, 0) < v:
                        self.pending[e][sid] = v

    def final_tokens(self):
        toks = {e: self.cnt[e] for e in ("pe", "act", "dve", "pool")}
        for j in range(self.NDS):
            toks["d%d" % j] = self.dval[j]
        return toks


def build_nc(n_layers=L, debug_stop=None):
    nc = bass.Bass("TRN2", target_bir_lowering=False)
    T = TK()

    def din(name, shape, dt):
        return nc.dram_tensor(name, list(shape), dt, kind="ExternalInput").ap()

    x_d = din("x", [S, D], F32)
    pp_d = din("pp", [128, L * NPP], F32)
    cb_d = din("cb16", [128, 2560], BF16)
    eall_d = din("eall", [2, 128, S], BF16)
    rc_d = din("rconst", [127, 33], BF16)
    fb_d = din("fbias", [128, NT * 32], F32)
    win_d = din("win", [L, 128, 8, 2828], F32)
    wg_d = din("wg", [L, 8, 128, 8 * 4 * 128], F32)
    wb_d = din("wb", [L, 8, 128, 8 * 128], F32)
    wo_d = din("wo", [L, 128, 8, 1024], F32)
    w1_d = din("w1", [L, 4, 128, 8, 1024], F32)
    w2_d = din("w2", [L, 4, 128, 8, 1024], F32)
    cw1_d = din("cw1", [L, 2, 128, 32, 128], F32)
    cw2_d = din("cw2", [L, 128, 128], F32)
    ws_d = din("ws", [L, 128, 512], F32)
    out_d = nc.dram_tensor("out", [S, D], F32, kind="ExternalOutput").ap()
    dbgy_d = None
    if debug_stop is not None:
        dbgy_d = nc.dram_tensor("dbgy", [128, 8 * S], BF16, kind="ExternalOutput").ap()

    es = contextlib.ExitStack()
    with es:
        def sbt(name, shape, dt):
            return es.enter_context(nc.sbuf_tensor(name, list(shape), dt))

        Xt = sbt("X", [128, NT * D], F32)
        XNt = sbt("XN", [128, 8 * S], BF16)
        Yt = sbt("Y", [128, 8 * S], BF16)
        Rt = sbt("R", [128, 30720], BF16)
        ppt = sbt("ppt", [128, L * NPP], F32)
        cbt = sbt("cbt", [128, 2560], BF16)
        fbt = sbt("fbt", [128, NT * 32], F32)
        Rht = sbt("Rh", [128, 2 * 97], BF16)
        KCt = sbt("KC", [128, 2 * 128], BF16)
        smt = sbt("small", [128, 256], F32)
        onest = sbt("ones256", [128, 128], F32)
        wsTt = sbt("wsT", [128, 512], BF16)
        nbtt = sbt("nbt", [128, 128], BF16)
        nbt2t = sbt("nbt2", [128, 128], BF16)
        sm2t = sbt("small2", [128, 256], F32)
        pst = [es.enter_context(nc.psum_tensor("ps%d" % i, [128, 512], F32)) for i in range(8)]
        sems = {}
        for e in TK.ENGS:
            sems[e] = es.enter_context(nc.semaphore("s_" + e))
        for j in range(TK.NDS):
            sems["d%d" % j] = es.enter_context(nc.semaphore("s_d%d" % j))

        X = Xt[:].rearrange("p (t d) -> p t d", t=NT)
        XN = XNt[:].rearrange("p (k s) -> p k s", k=8)
        Y = Yt[:].rearrange("p (k s) -> p k s", k=8)
        pp = ppt[:]
        cb = cbt[:]
        ident = cb[:, 0:128]
        M_le = cb[:, 128:256]
        M_ge = cb[:, 256:384]
        M_gt = cb[:, 384:512]
        cmpmask = cb[:, 512:2560]
        fb = fbt[:].rearrange("p (t j) -> p t j", t=NT)
        Rh = Rht[:].rearrange("p (h c) -> p h c", h=2)
        KC = KCt[:].rearrange("p (h c) -> p h c", h=2)
        sm = smt[:]
        ones256 = onest[:]
        wsT = wsTt[:].rearrange("p (g t) -> p g t", g=4)
        nbt = nbtt[:]
        sm2 = sm2t[:]
        c_nb = [nbtt[:], nbt2t[:]]
        c_rs = [sm2[:, 0:4], sm2[:, 4:8]]
        c_ri = [sm2[:, 8:12], sm2[:, 12:16]]
        c_cf = [sm2[:, 16:20], sm2[:, 20:24]]
        c_t8 = [[sm2[:, 32:40], sm2[:, 40:48]], [sm2[:, 48:56], sm2[:, 56:64]]]
        c_sc = [[sm2[:, 64:96], sm2[:, 96:128]], [sm2[:, 128:160], sm2[:, 160:192]]]

        def ps(i):
            return pst[i][:]

        def psb(i):
            return pst[i][:].bitcast(BF16)

        def R(off, nel, dt):
            if dt == BF16:
                return Rt[:, off // 2: off // 2 + nel]
            return Rt[:, off // 2: off // 2 + 2 * nel].bitcast(F32)

        def PK(i):
            return ("ps", i)

        def xnk(c):
            return [("XN", t) for t in range(4 * c, 4 * c + 4)]

        def yk(slot, c):
            return [("Y", slot, t) for t in range(4 * c, 4 * c + 4)]

        rots = {}

        def rot(name, items):
            i = rots.get(name, 0)
            rots[name] = i + 1
            return items[i % len(items)]

        def mmg(out, pairs, reads, writes):
            def fn(e):
                n = len(pairs)
                ins = None
                for i, (l, r) in enumerate(pairs):
                    ins = e.matmul(out, l, r, start=(i == 0), stop=(i == n - 1))
                return ins
            T.op("pe", fn, reads, writes)

        def mm1(out, l, r, start, stop, reads, writes, skip=False):
            T.op("pe", lambda e: e.matmul(out, l, r, start=start, stop=stop, skip_group_check=skip), reads, writes)

        def tps(outs_ins, idn, reads, writes):
            def fn(e):
                ins = None
                for o, i in outs_ins:
                    ins = e.transpose(o, i, idn)
                return ins
            T.op("pe", fn, reads, writes)

        def act(out, in_, func, reads, writes, bias=None, scale=None, accum=None):
            kw = {}
            if bias is not None:
                kw["bias"] = bias
            if scale is not None:
                kw["scale"] = scale
            if accum is not None:
                kw["accum_out"] = accum
            T.op("act", lambda e: e.activation(out, in_, func, **kw), reads, writes)

        def ts(out, in0, s1, s2, op0, op1, reads, writes, eng="dve"):
            if op1 is None:
                T.op(eng, lambda e: e.tensor_scalar(out, in0, s1, None, op0), reads, writes)
            else:
                T.op(eng, lambda e: e.tensor_scalar(out, in0, s1, s2, op0, op1), reads, writes)

        def tt(out, in0, in1, op, reads, writes, eng="dve"):
            T.op(eng, lambda e: e.tensor_tensor(out, in0, in1, op), reads, writes)

        def stt(out, in0, sc, in1, op0, op1, reads, writes):
            T.op("dve", lambda e: e.scalar_tensor_tensor(out, in0, sc, in1, op0, op1), reads, writes)

        def cpy(out, in_, reads, writes, eng="dve"):
            if eng == "act":
                act(out, in_, AF.Copy, reads, writes)
            else:
                T.op(eng, lambda e: e.tensor_copy(out, in_), reads, writes)

        def recip(out, in_, reads, writes):
            T.op("dve", lambda e: e.reciprocal(out, in_), reads, writes)

        def mset(ap, val, writes, eng="dve"):
            T.op(eng, lambda e: e.memset(ap, val), (), writes)

        def dma(eng, out, in_, reads, writes):
            def fn(e, sem):
                e.dma_start(out=out, in_=in_).then_inc(sem, 16)
            T.dma(eng, fn, 1, reads, writes)

        def wload(out, in_, writes):
            dma("pool", out, in_, (), writes)

        def wload_flat(off, nel, in_flat, writes):
            o = R(off, nel, BF16)
            if nel > 2048:
                o = o.rearrange("p (a b) -> p a b", b=2048)
                in_flat = in_flat.rearrange("p (a b) -> p a b", b=2048)
            dma("pool", o, in_flat, (), writes)

        dma("sp", ppt[:], pp_d[:, :], (), [("pp",)])
        dma("sp", cbt[:], cb_d[:, :], (), [("cb",)])
        dma("sp", fbt[:], fb_d[:, :], (), [("fb",)])
        for h in range(2):
            dma("sp", Rh[0:127, h, 64:97], rc_d[:, :], (), [("Rh", h)])
        for t in range(NT):
            dma("sp", X[:, t, :], x_d[t * 128:(t + 1) * 128, :], (), [("X", t)])
        mset(ones256, 1.0 / 256.0, [("ones",)])
        mset(KCt[:], 0.0, [("KC", 0), ("KC", 1)])
        mset(nbt, 0.0, [("cnb", 0)])
        mset(nbt2t[:], 0.0, [("cnb", 1)])

        def store_dbg():
            T.barrier()
            tok = None
            for t in range(NT):
                dma("sp", out_d[t * 128:(t + 1) * 128, :], X[:, t, :], [("X", t)], [("out", t)])
            dma("sp", dbgy_d[:, :], Yt[:], [("Y", s_, t) for s_ in range(8) for t in range(NT)], [("dbgy",)])

        def norm_phase(l, gcol):
            o_xs = [53248, 55296]
            o_junk = 57344
            junk = R(o_junk, 1024, BF16)
            ss = sm[:, 0:16]
            sd = sm[:, 16:32]
            rstd = sm[:, 32:48]
            for t in range(NT):
                act(junk, X[:, t, :], AF.Square, [("X", t)], [("junk",), ("ss", t)], accum=ss[:, t:t + 1])
            act(sd, ss, AF.Sqrt, [("ss", t) for t in range(NT)], [("sd",)], bias=1e-6, scale=1.0 / D)
            recip(rstd, sd, [("sd",)], [("rstd",)])
            g = pp[:, l * NPP + gcol: l * NPP + gcol + 8]
            for t in range(NT):
                xs = R(o_xs[t % 2], 1024, BF16)
                bk = 6 + (t % 2)
                act(xs, X[:, t, :], AF.Copy, [("X", t), ("rstd",)], [("xs", t % 2)], scale=rstd[:, t:t + 1])
                pv = psb(bk).rearrange("p (k s) -> p k s", k=8)
                tps([(pv[:, k, :], xs[:, k * 128:(k + 1) * 128]) for k in range(8)], ident,
                    [("xs", t % 2), ("cb",)], [PK(bk)])
                tt(XN[:, :, t * 128:(t + 1) * 128], pv, g.unsqueeze(2).broadcast_to([128, 8, 128]), ALU.mult,
                   [PK(bk), ("pp",)], [("XN", t)])

        def nsa_compress(l):
            T.barrier()
            wbuf0 = R(0, 4096, BF16).rearrange("p (k c) -> p k c", k=8)
            o_kcsb, o_X, o_w1 = 16384, 24576, 32768
            kcsb = R(o_kcsb, 2048, BF16)
            peb = R(o_X, 64, BF16)
            cbh = sm[:, 54:56]
            w1sb = R(o_w1, 4096, BF16).rearrange("p (l e) -> p l e", l=32)
            hid = [R(40960 + 256 * h, 127, BF16) for h in range(2)]
            ktok = R(41472, 128, BF16)
            w2sb = R(41728, 128, BF16).rearrange("p (k c) -> p k c", k=2)
            junkc = R(41984, 64, F32)
            kss = sm[:, 48:50]
            ksd = sm[:, 50:52]
            krs = sm[:, 52:54]
            wload(w2sb, cw2_d[l, :, :].rearrange("p (k c) -> p k c", k=2), [("w2sb",)])
            for kv in range(2):
                wload_flat(o_w1, 4096, cw1_d[l, kv, :, :, :].rearrange("p l e -> p (l e)"), [("w1sb",)])
                for c in range(4):
                    bk = rot("b2a", [0, 1])
                    mmg(ps(bk), [(wbuf0[:, k, kv * 128:(kv + 1) * 128], XN[:, k, c * 512:(c + 1) * 512]) for k in range(8)],
                        [("wbuf", 0)] + xnk(c), [PK(bk)])
                    cpy(kcsb[:, c * 512:(c + 1) * 512], ps(bk), [PK(bk)], [("kcsb", c)], eng="act")
                if kv == 1:
                    wload(wbuf0, win_d[l, :, :, W_BT1:W_BT1 + 512], [("wbuf", 0)])
                if kv == 0:
                    cpy(peb, pp[:, l * NPP + C_PE: l * NPP + C_PE + 64], [("pp",)], [("peb",)], eng="dve")
                for h in range(2):
                    mmg(ps(7)[:, h:h + 1], [(w1sb[h * 64:(h + 1) * 64, l_, :], peb[h * 64:(h + 1) * 64, kv * 32 + l_: kv * 32 + l_ + 1])
                                           for l_ in range(32)], [("w1sb",), ("peb",)], [PK(7)])
                    cpy(cbh[:, h:h + 1], ps(7)[:, h:h + 1], [PK(7)], [("cbh", h)], eng="dve")
                for h in range(2):
                    bh = 2 + h
                    mmg(ps(bh)[:, 0:127], [(w1sb[h * 64:(h + 1) * 64, l_, :], kcsb[h * 64:(h + 1) * 64, l_: l_ + 2017: 16]) for l_ in range(32)],
                        [("w1sb",)] + [("kcsb", c) for c in range(4)], [PK(bh)])
                    act(hid[h], ps(bh)[:, 0:127], AF.Gelu_apprx_tanh, [PK(bh), ("cbh", h)], [("hid", h)], bias=cbh[:, h:h + 1], scale=1.0)
                    bq = 4 + h
                    mm1(ps(bq)[0:127, 0:64], hid[h], w2sb[:, kv, :], True, True, [("hid", h), ("w2sb",)], [PK(bq)])
                    if kv == 0:
                        act(junkc[0:127, :], ps(bq)[0:127, 0:64], AF.Square, [PK(bq)], [("junkc",), ("kss", h)],
                            accum=kss[0:127, h:h + 1])
                    else:
                        cpy(Rh[0:127, h, 0:64], ps(bq)[0:127, 0:64], [PK(bq)], [("Rh", h)], eng="act")
                if kv == 0:
                    act(ksd[0:127, :], kss[0:127, :], AF.Sqrt, [("kss", 0), ("kss", 1)], [("ksd",)], bias=1e-6, scale=1.0 / 64)
                    recip(krs[0:127, :], ksd[0:127, :], [("ksd",)], [("krs",)])
                    for h in range(2):
                        ts(ktok[0:127, h * 64:(h + 1) * 64], ps(4 + h)[0:127, 0:64], krs[0:127, h:h + 1], None, ALU.mult, None,
                           [PK(4 + h), ("krs",)], [("ktok",)])
                    pT = psb(6)
                    tps([(pT[:, 0:127], ktok[0:127, :])], ident[0:127, 0:127], [("ktok",), ("cb",)], [PK(6)])
                    gk = pp[:, l * NPP + C_GK: l * NPP + C_GK + 1]
                    ts(KC[0:64, 0, 0:127], pT[0:64, 0:127], gk[0:64, :], None, ALU.mult, None, [PK(6), ("pp",)], [("KC", 0)])
                    ts(KC[64:128, 1, 0:127], pT[64:128, 0:127], gk[64:128, :], None, ALU.mult, None, [PK(6), ("pp",)], [("KC", 1)])

        QSLOT = [0, 1, 4, 5]
        o_KW = [16384, 20480]
        o_vs, o_vw = 24576, 28800
        o_gs = 41984
        o_E = [42752, 43776, 44800, 49920 + 4096, 51968, 52992, 55040, 56064]
        o_yb = 45824
        o_sq, o_qn = 49920, 51968

        def KWt(h):
            return R(o_KW[h], 2048, BF16)

        def vaug(o):
            return R(o, NT * 2 * 65, BF16).rearrange("p (t h c) -> p t h c", t=NT, h=2)

        def nsa_proj(l):
            T.barrier()
            wbuf0 = R(0, 4096, BF16).rearrange("p (k c) -> p k c", k=8)
            wbuf1 = R(8192, 4096, BF16).rearrange("p (k c) -> p k c", k=8)
            vs_, vw_ = vaug(o_vs), vaug(o_vw)
            gs = R(o_gs, 192, F32)
            sq = R(o_sq, 512, F32)
            qn = R(o_qn, 512, BF16)
            for h in range(2):
                dma("sp", Y[:, 6 + h, :], eall_d[h, :, :], (), [("Y", 6 + h, t) for t in range(NT)])
                mset(KWt(h), 0.0, [("KW", h, t) for t in range(NT)])
            for s_ in QSLOT:
                mset(Y[:, s_, :], 0.0, [("Y", s_, t) for t in range(NT)])
            mset(vs_[:, :, :, 64:65], 1.0, [("vs", t) for t in range(NT)])
            mset(vw_[:, :, :, 64:65], 1.0, [("vw", t) for t in range(NT)])
            if BIS == 1:
                return
            ssq = sm[:, 64:72]
            sd8 = sm[:, 72:80]
            r8 = sm[:, 80:88]
            gq = pp[:, l * NPP + C_GQ: l * NPP + C_GQ + 1]
            gks = pp[:, l * NPP + C_GK + 1: l * NPP + C_GK + 2]
            gkw = pp[:, l * NPP + C_GK + 2: l * NPP + C_GK + 3]
            qnb = [R(o_qn, 512, BF16), R(o_qn + 1024, 512, BF16)]

            ssqb = [sm[:, 64:72], sm[:, 240:248]]

            def stageA1(t):
                ba, bb = [0, 1, 6][t % 3], [2, 3, 7][t % 3]
                tc_ = slice(t * 128, (t + 1) * 128)
                ssq = ssqb[t % 2]
                mmg(ps(ba), [(XN[:, k, tc_], wbuf0[:, k, :]) for k in range(8)], [("XN", t), ("wbuf", 0)], [PK(ba)])
                mmg(ps(bb)[:, 0:268], [(XN[:, k, tc_], wbuf1[:, k, 0:268]) for k in range(8)], [("XN", t), ("wbuf", 1)], [PK(bb)])
                act(sq[:, 0:384], ps(ba)[:, 0:384], AF.Square, [PK(ba)], [("sq", 0)])
                act(sq[:, 384:512], ps(bb)[:, 0:128], AF.Square, [PK(bb)], [("sq", 1)])
                T.op("dve", lambda e, sq=sq, ssq=ssq: e.tensor_reduce(ssq, sq.rearrange("p (h d) -> p h d", h=8), AX.X, ALU.add),
                     [("sq", 0), ("sq", 1)], [("ssq", t % 2)])

            def stageA(t):
                ba, bb = [0, 1, 6][t % 3], [2, 3, 7][t % 3]
                qn = qnb[t % 2]
                qi = t % 2
                tc_ = slice(t * 128, (t + 1) * 128)
                ssq = ssqb[t % 2]
                act(sd8, ssq, AF.Sqrt, [("ssq", t % 2)], [("sd8",)], bias=1e-6, scale=1.0 / 64)
                recip(r8, sd8, [("sd8",)], [("r8",)])
                tt(qn[:, 0:256].rearrange("p (j i d) -> p i j d", j=2, i=2), ps(ba)[:, 0:256].rearrange("p (i j d) -> p i j d", i=2, j=2),
                   r8[:, 0:4].rearrange("p (i j) -> p i j", i=2).unsqueeze(3).broadcast_to([128, 2, 2, 64]), ALU.mult,
                   [PK(ba), ("r8",)], [("qn", qi, 0)])
                tt(qn[:, 256:384].rearrange("p (h d) -> p h d", h=2), ps(ba)[:, 256:384].rearrange("p (h d) -> p h d", h=2),
                   r8[:, 4:6].unsqueeze(2).broadcast_to([128, 2, 64]), ALU.mult, [PK(ba), ("r8",)], [("qn", qi, 2)])
                tt(qn[:, 384:512].rearrange("p (h d) -> p h d", h=2), ps(bb)[:, 0:128].rearrange("p (h d) -> p h d", h=2),
                   r8[:, 6:8].unsqueeze(2).broadcast_to([128, 2, 64]), ALU.mult, [PK(bb), ("r8",)], [("qn", qi, 1)])
                cpy(vs_[:, t, :, 0:64], ps(ba)[:, 384:512].rearrange("p (h d) -> p h d", h=2), [PK(ba)], [("vs", t)], eng="act")
                cpy(vw_[:, t, :, 0:64], ps(bb)[:, 128:256].rearrange("p (h d) -> p h d", h=2), [PK(bb)], [("vw", t)], eng="act")
                cpy(gs[:, t * 12:(t + 1) * 12], ps(bb)[:, 256:268], [PK(bb)], [("gs",)], eng="act")

            def stageB(t):
                bt = [4, 5][t % 2]
                qn = qnb[t % 2]
                qi = t % 2
                tc_ = slice(t * 128, (t + 1) * 128)
                pT = psb(bt).rearrange("p (a s) -> p a s", a=8)
                tps([(pT[:, 0, :], qn[:, 0:128]), (pT[:, 1, :], qn[:, 128:256]),
                     (pT[:, 2, :], qn[:, 256:384]), (pT[:, 3, :], qn[:, 384:512])], ident,
                    [("qn", qi, 0), ("qn", qi, 1), ("qn", qi, 2), ("cb",)], [PK(bt)])
                act(Y[0:64, QSLOT[0], tc_], pT[0:64, 0, :], AF.Copy, [PK(bt), ("pp",)], [("Y", QSLOT[0], t)], scale=gq[0:64, :])
                act(Y[64:128, QSLOT[2], tc_], pT[64:128, 0, :], AF.Copy, [PK(bt), ("pp",)], [("Y", QSLOT[2], t)], scale=gq[64:128, :])
                act(Y[0:64, QSLOT[1], tc_], pT[0:64, 1, :], AF.Copy, [PK(bt), ("pp",)], [("Y", QSLOT[1], t)], scale=gq[0:64, :])
                act(Y[64:128, QSLOT[3], tc_], pT[64:128, 1, :], AF.Copy, [PK(bt), ("pp",)], [("Y", QSLOT[3], t)], scale=gq[64:128, :])
                ts(Y[0:64, 6, tc_], pT[0:64, 2, :], gks[0:64, :], None, ALU.mult, None, [PK(bt), ("pp",)], [("Y", 6, t)])
                ts(Y[64:128, 7, tc_], pT[64:128, 2, :], gks[64:128, :], None, ALU.mult, None, [PK(bt), ("pp",)], [("Y", 7, t)])
                ts(KWt(0)[0:64, tc_], pT[0:64, 3, :], gkw[0:64, :], None, ALU.mult, None, [PK(bt), ("pp",)], [("KW", 0, t)])
                ts(KWt(1)[64:128, tc_], pT[64:128, 3, :], gkw[64:128, :], None, ALU.mult, None, [PK(bt), ("pp",)], [("KW", 1, t)])

            stageA1(0)
            stageA1(1)
            stageA(0)
            for t in range(NT):
                if t + 2 < NT:
                    stageA1(t + 2)
                if t + 1 < NT:
                    stageA(t + 1)
                stageB(t)
            wload(wbuf0, win_d[l, :, :, W_A:W_A + 512], [("wbuf", 0)])
            wload(wbuf1, win_d[l, :, :, W_C:W_C + 512], [("wbuf", 1)])
            act(gs, gs, AF.Sigmoid, [("gs",)], [("gs",)])

        def nsa_attn(l):
            vs_, vw_ = vaug(o_vs), vaug(o_vw)
            gs = R(o_gs, 192, F32).rearrange("p (t g) -> p t g", t=NT)
            yb = R(o_yb, 1024, F32).rearrange("p (q c) -> p q c", q=4)
            ybb = R(o_sq, 256, BF16)
            tmp = R(o_sq + 1024, 256, F32).rearrange("p (q c) -> p q c", q=4)
            rs4 = sm[:, 96:100]
            ri4 = sm[:, 100:104]
            cf4 = sm[:, 104:108]
            top8 = sm[:, 112:120]
            sc = sm[:, 128:160]
            def comb_T(c, qt, ybbs):
                t = 4 * c + qt
                bk = [6, 7][qt % 2]
                yq = ybbs[qt % 2]
                pT = psb(bk).rearrange("p (a s) -> p a s", a=8)
                tps([(pT[:, 0, :], yq[:, 0:128]), (pT[:, 1, :], yq[:, 128:256])], ident, [("ybb", qt % 2), ("cb",)], [PK(bk)])
                cpy(Y[:, 2:4, t * 128:(t + 1) * 128], pT[:, 0:2, :], [PK(bk)], [("Y", 2, t), ("Y", 3, t)], eng="dve")

            for c in range(4):
                ch = slice(c * 512, (c + 1) * 512)
                Es = []
                for hq in range(4):
                    h = hq // 2
                    bs_ = rot("aS", [0, 1])
                    Ei = hq
                    E = R(o_E[Ei], 512, BF16)
                    mm1(ps(bs_)[0:127, :], KC[:, h, 0:127], Y[:, QSLOT[hq], ch], True, True,
                        [("KC", h)] + yk(QSLOT[hq], c), [PK(bs_)])
                    act(E[0:127, :], ps(bs_)[0:127, :], AF.Exp, [PK(bs_)], [("E", Ei)], scale=0.125)
                    tt(E[0:127, :], E[0:127, :], cmpmask[0:127, ch], ALU.mult, [("E", Ei), ("cb",)], [("E", Ei)])
                    Es.append(E)
                for qp in range(2):
                    qts = [2 * qp, 2 * qp + 1]
                    pcl = []
                    for qi, qt in enumerate(qts):
                        bc = [4, 5][qi]
                        pc = ps(bc)[:, 0:388].rearrange("p (q c) -> p q c", q=4)
                        for hq in range(4):
                            mm1(pc[:, hq, :], Es[hq][0:127, qt * 128:(qt + 1) * 128], Rh[0:127, hq // 2, :], True, True,
                                [("E", hq), ("Rh", hq // 2)], [PK(bc)])
                        pcl.append((bc, pc))
                    for qi, qt in enumerate(qts):
                        bc, pc = pcl[qi]
                        ts(c_rs[qi], pc[:, :, 64], 1e-30, None, ALU.max, None, [PK(bc)], [("crs", qi)])
                    for qi, qt in enumerate(qts):
                        recip(c_ri[qi], c_rs[qi], [("crs", qi)], [("cri", qi)])
                    for h in range(2):
                        for qi, qt in enumerate(qts):
                            bc, pc = pcl[qi]
                            stt(c_sc[qi][h], pc[:, 2 * h, 65:97], c_ri[qi][:, 2 * h:2 * h + 1], fb[:, 4 * c + qt, :], ALU.mult, ALU.add,
                                [PK(bc), ("cri", qi), ("fb",)], [("csc", qi, h)])
                    for h in range(2):
                        for qi, qt in enumerate(qts):
                            bc, pc = pcl[qi]
                            stt(c_sc[qi][h], pc[:, 2 * h + 1, 65:97], c_ri[qi][:, 2 * h + 1:2 * h + 2], c_sc[qi][h], ALU.mult, ALU.add,
                                [PK(bc), ("cri", qi), ("csc", qi, h)], [("csc", qi, h)])
                    for h in range(2):
                        for qi, qt in enumerate(qts):
                            T.op("dve", lambda e, o=c_t8[qi][h], i_=c_sc[qi][h]: e.max(o, i_), [("csc", qi, h)], [("ct8", qi, h)])
                    for h in range(2):
                        col = 64 if h == 0 else 0
                        for qi, qt in enumerate(qts):
                            ts(c_nb[qi][:, col:col + 32], c_sc[qi][h], c_t8[qi][h][:, 7:8], NEGB, ALU.is_lt, ALU.mult,
                               [("csc", qi, h), ("ct8", qi, h)], [("cnb", qi)])
                    for qi, qt in enumerate(qts):
                        tt(c_cf[qi], gs[:, 4 * c + qt, 0:12:3], c_ri[qi], ALU.mult, [("gs",), ("cri", qi)], [("ccf", qi)])
                    for qi, qt in enumerate(qts):
                        bc, pc = pcl[qi]
                        tt(yb[:, qt, :].rearrange("p (h d) -> p h d", h=4), pc[:, :, 0:64],
                           c_cf[qi].unsqueeze(2).broadcast_to([128, 4, 64]), ALU.mult, [PK(bc), ("ccf", qi)], [("yb", qt)])
                    for qi, qt in enumerate(qts):
                        t = 4 * c + qt
                        tc_ = slice(t * 128, (t + 1) * 128)
                        bT = [6, 7][qi]
                        pT = psb(bT)
                        tps([(pT[:, 0:128], c_nb[qi])], ident, [("cnb", qi), ("cb",)], [PK(bT)])
                        cpy(Y[0:32, QSLOT[2], tc_], pT[0:32, 0:128], [PK(bT)], [("Y", QSLOT[2], t)], eng="act")
                        cpy(Y[0:32, QSLOT[3], tc_], pT[0:32, 0:128], [PK(bT)], [("Y", QSLOT[3], t)], eng="act")
                        cpy(Y[64:96, QSLOT[0], tc_], pT[64:96, 0:128], [PK(bT)], [("Y", QSLOT[0], t)], eng="dve")
                        cpy(Y[64:96, QSLOT[1], tc_], pT[64:96, 0:128], [PK(bT)], [("Y", QSLOT[1], t)], eng="dve")
                items = []
                for br in range(2):
                    for hq in range(4):
                        bo = rot("aO", [2, 3])
                        kts = list(range(0, 4 * c + 4)) if br == 0 else list(range(max(0, 4 * c - 4), 4 * c + 4))
                        for ix, kt in enumerate(kts):
                            n_ = len(items)
                            items.append(dict(br=br, hq=hq, h=hq // 2, bo=bo, kt=kt, first=(ix == 0), lastg=(ix == len(kts) - 1),
                                              bs=[0, 1, 7, 6, 4, 5][n_ % 6], Ei=n_ % 8))

                def geom(it):
                    kt, br = it["kt"], it["br"]
                    qlo = max(kt, 4 * c)
                    qhi = 4 * c + 3 if br == 0 else min(kt + 4, 4 * c + 3)
                    return qlo, qhi, (qhi - qlo + 1) * 128

                def emit_S(it):
                    qlo, qhi, N = geom(it)
                    kt, br, hq, h, bs_ = it["kt"], it["br"], it["hq"], it["h"], it["bs"]
                    qc = slice(qlo * 128, (qhi + 1) * 128)
                    kc_ = slice(kt * 128, (kt + 1) * 128)
                    qkeys = [("Y", QSLOT[hq], t_) for t_ in range(qlo, qhi + 1)]
                    if br == 0:
                        mm1(ps(bs_)[:, 0:N], Y[:, 6 + h, kc_], Y[:, QSLOT[hq], qc], True, True,
                            [("Y", 6 + h, kt)] + qkeys, [PK(bs_)])
                    else:
                        mm1(ps(bs_)[:, 0:N], KWt(h)[:, kc_], Y[:, QSLOT[hq], qc], True, True,
                            [("KW", h, kt)] + qkeys, [PK(bs_)])

                def emit_exp(it):
                    qlo, qhi, N = geom(it)
                    kt, br, bs_, Ei = it["kt"], it["br"], it["bs"], it["Ei"]
                    E = R(o_E[Ei], 512, BF16)
                    act(E[:, 0:N], ps(bs_)[:, 0:N], AF.Exp, [PK(bs_)], [("E", Ei)], scale=0.125)
                    if kt >= 4 * c:
                        tt(E[:, 0:128], E[:, 0:128], M_le, ALU.mult, [("E", Ei), ("cb",)], [("E", Ei)])
                    if br == 1 and kt + 4 <= 4 * c + 3:
                        tt(E[:, N - 128:N], E[:, N - 128:N], M_gt, ALU.mult, [("E", Ei), ("cb",)], [("E", Ei)])

                def emit_PV(it):
                    qlo, qhi, N = geom(it)
                    kt, br, hq, h, bo, Ei = it["kt"], it["br"], it["hq"], it["h"], it["bo"], it["Ei"]
                    E = R(o_E[Ei], 512, BF16)
                    po = ps(bo)[:, 0:260].rearrange("p (q c) -> p q c", q=4)
                    va = vs_ if br == 0 else vw_
                    vkey = ("vs", kt) if br == 0 else ("vw", kt)
                    first = it["first"]
                    for qt_ in range(qlo, qhi + 1):
                        lastmm = (it["lastg"] and qt_ == qhi)
                        mm1(po[:, qt_ - 4 * c, :], E[:, (qt_ - qlo) * 128:(qt_ - qlo + 1) * 128], va[:, kt, h, :],
                            first, lastmm, [("E", Ei), vkey], [PK(bo)], skip=True)
                        first = False
                    if it["lastg"]:
                        deferred.append([3, lambda po=po, bo=bo, hq=hq, br=br: evac(po, bo, hq, br)])

                def evac(po, bo, hq, br):
                    if True:
                        recip(ri4, po[:, :, 64], [PK(bo)], [("ri4",)])
                        gcol = hq * 3 + 1 + br
                        tt(cf4, gs[:, 4 * c:4 * c + 4, gcol], ri4, ALU.mult, [("gs",), ("ri4",)], [("cf4",)])
                        tt(tmp, po[:, :, 0:64], cf4.unsqueeze(2).broadcast_to([128, 4, 64]), ALU.mult, [PK(bo), ("cf4",)], [("tmpa",)])
                        tt(yb[:, :, hq * 64:(hq + 1) * 64], yb[:, :, hq * 64:(hq + 1) * 64], tmp, ALU.add,
                           [("tmpa",)] + [("yb", q) for q in range(4)], [("yb", q) for q in range(4)])

                LOOK = 5
                deferred = []
                for i in range(min(LOOK, len(items))):
                    emit_S(items[i])
                for i in range(len(items)):
                    emit_exp(items[i])
                    if i + LOOK < len(items):
                        emit_S(items[i + LOOK])
                    for d_ in deferred:
                        d_[0] -= 1
                    while deferred and deferred[0][0] <= 0:
                        deferred.pop(0)[1]()
                    emit_PV(items[i])
                while deferred:
                    deferred.pop(0)[1]()
                ybbs = [ybb, R(o_sq + 512, 256, BF16)]
                for qt in range(4):
                    cpy(ybbs[qt % 2], yb[:, qt, :], [("yb", qt)], [("ybb", qt % 2)], eng="act")
                    if qt >= 1:
                        comb_T(c, qt - 1, ybbs)
                comb_T(c, 3, ybbs)

        def gmlp(l):
            T.barrier()
            wbuf0 = R(0, 4096, BF16).rearrange("p (k c) -> p k c", k=8)
            u = R(16384, NT * 256, BF16).rearrange("p (t c) -> p t c", t=NT)
            v = R(16384 + 8192, NT * 256, F32).rearrange("p (t c) -> p t c", t=NT)
            o2 = 16384 + 8192 + 16384
            wsf = R(o2, 512, F32)
            wsm = R(o2 + 2048, 512, BF16)
            vn = R(o2 + 3072, 256, F32)
            vb = R(o2 + 4096, 256, BF16)
            ya = R(o2 + 4608, 256, BF16)
            st6 = sm[:, 160:166]
            mv = sm[:, 168:200].rearrange("p (t a) -> p t a", t=NT)
            sd16 = sm[:, 200:216]
            rs16 = sm[:, 216:232]
            dma("sp", wsf, ws_d[l, :, :], (), [("wsf",)])
            tt(wsm.rearrange("p (g s) -> p g s", g=4), wsf.rearrange("p (g s) -> p g s", g=4),
               M_ge.unsqueeze(1).broadcast_to([128, 4, 128]), ALU.mult, [("wsf",), ("cb",)], [("wsm",)])
            pT = psb(6).rearrange("p (a s) -> p a s", a=8)
            tps([(pT[:, g, :], wsm[:, g * 128:(g + 1) * 128]) for g in range(4)], ident, [("wsm",), ("cb",)], [PK(6)])
            cpy(wsT, pT[:, 0:4, :], [PK(6)], [("wsT",)], eng="dve")
            for t in range(NT):
                ba = rot("ga", [0, 1])
                tc_ = slice(t * 128, (t + 1) * 128)
                mmg(ps(ba), [(XN[:, k, tc_], wbuf0[:, k, :]) for k in range(8)], [("XN", t), ("wbuf", 0)], [PK(ba)])
                act(u[:, t, :], ps(ba)[:, 0:256], AF.Gelu_apprx_tanh, [PK(ba)], [("u", t)])
                act(v[:, t, :], ps(ba)[:, 256:512], AF.Gelu_apprx_tanh, [PK(ba)], [("v", t)])
                T.op("dve", lambda e, t=t: e.bn_stats(st6, v[:, t, :]), [("v", t)], [("st6",)])
                T.op("dve", lambda e, t=t: e.bn_aggr(mv[:, t, :], st6), [("st6",)], [("mv", t)])
            wload(wbuf0, win_d[l, :, :, W_D:W_D + 512], [("wbuf", 0)])
            act(sd16, mv[:, :, 1], AF.Sqrt, [("mv", t) for t in range(NT)], [("sd16",)], bias=1e-5, scale=1.0)
            recip(rs16, sd16, [("sd16",)], [("rs16",)])
            lng = pp[:, l * NPP + C_LNG: l * NPP + C_LNG + 256]
            lnb = pp[:, l * NPP + C_LNB: l * NPP + C_LNB + 256]
            bsT = pp[:, l * NPP + C_BS: l * NPP + C_BS + 4]
            vbb = [vb, R(o2 + 5120, 256, BF16)]
            yab = [ya, R(o2 + 5632, 256, BF16)]

            def g1(t):
                vb_ = vbb[t % 2]
                bm = [2, 3][t % 2]
                stt(vn, v[:, t, :], mv[:, t, 0:1], lng, ALU.subtract, ALU.mult, [("v", t), ("mv", t), ("pp",)], [("vn",)])
                stt(vb_, vn, rs16[:, t:t + 1], lnb, ALU.mult, ALU.add, [("vn",), ("rs16",), ("pp",)], [("vb", t % 2)])
                for g in range(4):
                    mm1(ps(bm)[:, g * 64:(g + 1) * 64], wsT[:, g, :], vb_[:, g * 64:(g + 1) * 64], True, True,
                        [("wsT",), ("vb", t % 2)], [PK(bm)])

            def g2(t):
                bm, bt = [2, 3][t % 2], [4, 5][t % 2]
                ya_ = yab[t % 2]
                for g in range(4):
                    stt(ya_[:, g * 64:(g + 1) * 64], ps(bm)[:, g * 64:(g + 1) * 64], bsT[:, g:g + 1], u[:, t, g * 64:(g + 1) * 64],
                        ALU.add, ALU.mult, [PK(bm), ("pp",), ("u", t)], [("ya", t % 2)])
                pT = psb(bt).rearrange("p (a s) -> p a s", a=8)
                tps([(pT[:, 0, :], ya_[:, 0:128]), (pT[:, 1, :], ya_[:, 128:256])], ident, [("ya", t % 2), ("cb",)], [PK(bt)])
                cpy(Y[:, 0:2, t * 128:(t + 1) * 128], pT[:, 0:2, :], [PK(bt)], [("Y", 0, t), ("Y", 1, t)], eng="act")

            g1(0)
            for t in range(NT):
                if t + 1 < NT:
                    g1(t + 1)
                g2(t)

        def conf(l):
            T.barrier()
            wbuf0 = R(8192, 4096, BF16).rearrange("p (k c) -> p k c", k=8)
            ZW = 30 + S
            zpad = R(16384, 2 * ZW, BF16).rearrange("p (c s) -> p c s", c=2)
            o2 = 16384 + 8320
            dg = R(o2, 2 * 31 * 128, BF16).rearrange("p (c j e) -> p c j e", c=2, j=31)
            o3 = o2 + 15872
            acc = R(o3, 1024, F32).rearrange("p (c s) -> p c s", c=2)
            sqc = R(o3 + 4096, 1024, F32).rearrange("p (c s) -> p c s", c=2)
            sg = R(o3 + 8192, 512, F32)
            msq = R(o3 + 10240, 512, F32)
            var = R(o3 + 12288, 512, F32)
            rstd = R(o3 + 14336, 512, F32)
            zc = R(o3 + 16384, 512, F32)
            base = l * NPP
            mset(zpad[:, :, 0:30], 0.0, [("zpad", -1)])
            for cc in range(2):
                cw = pp[:, base + C_CW + cc * 31: base + C_CW + (cc + 1) * 31]
                tt(dg[:, cc, :, :], ident.unsqueeze(1).broadcast_to([128, 31, 128]), cw.unsqueeze(2).broadcast_to([128, 31, 128]),
                   ALU.mult, [("cb",), ("pp",)], [("dg", cc)])
            def cproj(c):
                ch = slice(c * 512, (c + 1) * 512)
                for cc in range(2):
                    ba, bb = cc, 2 + cc
                    mmg(ps(ba), [(wbuf0[:, k, cc * 128:(cc + 1) * 128], XN[:, k, ch]) for k in range(8)], [("wbuf", 1)] + xnk(c), [PK(ba)])
                    mmg(ps(bb), [(wbuf0[:, k, 256 + cc * 128:256 + (cc + 1) * 128], XN[:, k, ch]) for k in range(8)],
                        [("wbuf", 1)] + xnk(c), [PK(bb)])

            sgs = [sg, R(o3 + 18432, 512, F32)]
            cproj(0)
            for c in range(4):
                ch = slice(c * 512, (c + 1) * 512)
                for cc in range(2):
                    ba, bb = cc, 2 + cc
                    act(sgs[cc], ps(bb), AF.Sigmoid, [PK(bb)], [("sg", cc)])
                    tt(zpad[:, cc, 30 + c * 512: 30 + (c + 1) * 512], ps(ba), sgs[cc], ALU.mult, [PK(ba), ("sg", cc)], [("zpad", c, cc)])
                for cc in range(2):
                    bcv = [6, 7][cc]
                    mmg(ps(bcv), [(dg[:, cc, j, :], zpad[:, cc, c * 512 + j: c * 512 + j + 512]) for j in range(31)],
                        [("dg", cc), ("zpad", c, cc), ("zpad", c - 1, cc), ("zpad", -1)], [PK(bcv)])
                for cc in range(2):
                    bcv = [6, 7][cc]
                    cbias = pp[:, base + C_CB + cc: base + C_CB + cc + 1]
                    act(acc[:, cc, :], ps(bcv), AF.Identity, [PK(bcv), ("pp",)], [("acc", cc)], bias=cbias, scale=1.0)
                    act(sqc[:, cc, :], acc[:, cc, :], AF.Square, [("acc", cc)], [("sqc", cc)])
                mmg(ps(4), [(ones256, acc[:, cc, :]) for cc in range(2)], [("ones",), ("acc", 0), ("acc", 1)], [PK(4)])
                mmg(ps(5), [(ones256, sqc[:, cc, :]) for cc in range(2)], [("ones",), ("sqc", 0), ("sqc", 1)], [PK(5)])
                if c < 3:
                    cproj(c + 1)
                act(msq, ps(4), AF.Square, [PK(4)], [("msq",)])
                tt(var, ps(5), msq, ALU.subtract, [PK(5), ("msq",)], [("var",)])
                act(var, var, AF.Sqrt, [("var",)], [("var",)], bias=1e-5, scale=1.0)
                recip(rstd, var, [("var",)], [("rstdc",)])
                for cc in range(2):
                    tt(zc, acc[:, cc, :], ps(4), ALU.subtract, [("acc", cc), PK(4)], [("zc",)])
                    tt(zc, zc, rstd, ALU.mult, [("zc",), ("rstdc",)], [("zc",)])
                    act(Y[:, 4 + cc, ch], zc, AF.Silu, [("zc",), ("pp",)], yk(4 + cc, c),
                        bias=pp[:, base + C_CLB + cc: base + C_CLB + cc + 1], scale=pp[:, base + C_CLG + cc: base + C_CLG + cc + 1])
            wload(wbuf0[:, :, 0:256], win_d[l, :, :, W_D + 512:W_D + 768], [("wbuf", 1)])

        def sconv(l):
            T.barrier()
            wbuf0 = R(0, 4096, BF16).rearrange("p (k c) -> p k c", k=8)
            wbuf1 = R(8192, 4096, BF16).rearrange("p (k c) -> p k c", k=8)
            MW = 2 + S
            mpad = R(16384, 2 * MW, F32).rearrange("p (c s) -> p c s", c=2)
            o2 = 16384 + 2 * MW * 4
            shs = R(o2, 512, F32)
            acc = R(o2 + 2048, 512, F32)
            base = l * NPP
            mset(mpad[:, :, 0:2], 0.0, [("mpad", -1)])
            wload_flat(32768 + 8192, 4096, wg_d[l, 0, :, :], [("wg", 1)])
            wload(R(49152 + 2048, 1024, BF16).rearrange("p (a e) -> p a e", a=8), wb_d[l, 0, :, :].rearrange("p (a e) -> p a e", a=8), [("wb", 1)])
            for c in range(4):
                ch = slice(c * 512, (c + 1) * 512)
                for cc in range(2):
                    bB, bC, bH = rot("dB", [0, 1]), rot("dC", [2, 3]), rot("dH", [4, 5])
                    cs_ = slice(cc * 128, (cc + 1) * 128)
                    mmg(ps(bB), [(wbuf0[:, k, cs_], XN[:, k, ch]) for k in range(8)], [("wbuf", 0)] + xnk(c), [PK(bB)])
                    mmg(ps(bC), [(wbuf0[:, k, 256 + cc * 128:256 + (cc + 1) * 128], XN[:, k, ch]) for k in range(8)],
                        [("wbuf", 0)] + xnk(c), [PK(bC)])
                    mmg(ps(bH), [(wbuf1[:, k, cs_], XN[:, k, ch]) for k in range(8)], [("wbuf", 1)] + xnk(c), [PK(bH)])
                    cpy(shs, ps(bH), [PK(bH)], [("shs",)], eng="act")
                    tt(mpad[:, cc, 2 + c * 512: 2 + (c + 1) * 512], ps(bC), shs, ALU.mult, [PK(bC), ("shs",)], [("mpad", c, cc)])
                    sw = pp[:, base + C_SW + cc * 3: base + C_SW + cc * 3 + 3]
                    mr = [("mpad", c, cc), ("mpad", c - 1, cc), ("mpad", -1), ("pp",)]
                    ts(acc, mpad[:, cc, c * 512: c * 512 + 512], sw[:, 0:1], None, ALU.mult, None, mr, [("accd",)])
                    for j in (1, 2):
                        stt(acc, mpad[:, cc, c * 512 + j: c * 512 + j + 512], sw[:, j:j + 1], acc, ALU.mult, ALU.add,
                            mr + [("accd",)], [("accd",)])
                    tt(Y[:, 6 + cc, ch], ps(bB), acc, ALU.mult, [PK(bB), ("accd",)], yk(6 + cc, c))

        def gate_phase(l):
            T.barrier()
            mixT = R(0, 8 * S, BF16).rearrange("p (j s) -> p j s", j=8)
            wgb = [R(32768 + 8192 * i, 4096, BF16).rearrange("p (k b e) -> p k b e", k=8, b=4) for i in range(2)]
            wbb = [R(49152 + 2048 * i, 1024, BF16).rearrange("p (a e) -> p a e", a=8) for i in range(2)]
            gsb = [R(53248 + 2048 * i, 512, F32) for i in range(2)]
            prod = [R(57344 + 1024 * i, 512, BF16) for i in range(4)]
            base = l * NPP
            gpend = []
            for j in range(8):
                wi = (j + 1) % 2
                if j > 0:
                    wload_flat(32768 + 8192 * wi, 4096, wg_d[l, j, :, :], [("wg", wi)])
                    wload(wbb[wi], wb_d[l, j, :, :].rearrange("p (a e) -> p a e", a=8), [("wb", wi)])
                for c in range(4):
                    ch = slice(c * 512, (c + 1) * 512)
                    for b in range(4):
                        bg, bp = rot("gG", [0, 1]), rot("gP", [2, 3])
                        gi = rot("gsb", [0, 1])
                        mmg(ps(bg), [(wgb[wi][:, k, b, :], XN[:, k, ch]) for k in range(8)], [("wg", wi)] + xnk(c), [PK(bg)])
                        act(gsb[gi], ps(bg), AF.Sigmoid, [PK(bg), ("pp",)], [("gsb", gi)],
                            bias=pp[:, base + C_BG + b * 8 + j: base + C_BG + b * 8 + j + 1], scale=1.0)
                        mmg(ps(bp), [(wbb[wi][:, b * 2 + cc, :], Y[:, b * 2 + cc, ch]) for cc in range(2)],
                            [("wb", wi)] + yk(b * 2, c) + yk(b * 2 + 1, c), [PK(bp)])
                        if b == 0 and gpend:
                            gpend.pop(0)()
                        tt(prod[b], ps(bp), gsb[gi], ALU.mult, [PK(bp), ("gsb", gi)], [("prod", b)])

                    def fin(j=j, c=c, ch=ch):
                        bm = rot("gM", [4, 5])
                        mmg(ps(bm), [(ident, prod[b]) for b in range(4)], [("cb",)] + [("prod", b) for b in range(4)], [PK(bm)])
                        cpy(mixT[:, j, ch], ps(bm), [PK(bm)], [("mix", j, c)], eng="act")
                    gpend.append(fin)
            while gpend:
                gpend.pop(0)()
            wo = R(32768, 8192, BF16).rearrange("p (j e) -> p j e", j=8)
            wload_flat(32768, 8192, wo_d[l, :, :, :].rearrange("p j e -> p (j e)"), [("wo",), ("wg", 0), ("wg", 1)])
            for t in range(NT):
                for hf in range(2):
                    bo = rot("gO", [6, 7, 0, 1])
                    mmg(ps(bo), [(mixT[:, j, t * 128:(t + 1) * 128], wo[:, j, hf * 512:(hf + 1) * 512]) for j in range(8)],
                        [("wo",)] + [("mix", j, t // 4) for j in range(8)], [PK(bo)])
                    tt(X[:, t, hf * 512:(hf + 1) * 512], X[:, t, hf * 512:(hf + 1) * 512], ps(bo), ALU.add,
                       [("X", t), PK(bo)], [("X", t)])

        def ffn(l, last):
            hid = Y
            w1b = R(0, 8192, BF16).rearrange("p (k f) -> p k f", k=8)
            w2b = R(16384, 8192, BF16).rearrange("p (k f) -> p k f", k=8)
            rt = [R(32768 + 2048 * i, 512, F32) for i in range(2)]
            for fblk in range(4):
                al1 = [("mix", j, c) for j in range(0, 4) for c in range(4)] if fblk == 0 else []
                al2 = [("mix", j, c) for j in range(4, 8) for c in range(4)] if fblk == 0 else []
                wload_flat(0, 8192, w1_d[l, fblk, :, :, :].rearrange("p k f -> p (k f)"), [("w1b",)] + al1)
                wload_flat(16384, 8192, w2_d[l, fblk, :, :, :].rearrange("p k f -> p (k f)"), [("w2b",)] + al2)
                for fc in range(8):
                    for c in range(4):
                        ch = slice(c * 512, (c + 1) * 512)
                        bh = rot("fH", [0, 1, 2, 3])
                        ri = rot("frt", [0, 1])
                        mmg(ps(bh), [(w1b[:, k, fc * 128:(fc + 1) * 128], XN[:, k, ch]) for k in range(8)], [("w1b",)] + xnk(c), [PK(bh)])
                        act(rt[ri], ps(bh), AF.Relu, [PK(bh)], [("rt", ri)] + ([("wo",)] if (fblk == 0 and fc == 0 and c < 2) else []))
                        tt(hid[:, fc, ch], rt[ri], rt[ri], ALU.mult, [("rt", ri)], yk(fc, c))
                for t in range(NT):
                    for hf in range(2):
                        bo = rot("fO", [4, 5, 6, 7])
                        mmg(ps(bo), [(hid[:, fc, t * 128:(t + 1) * 128], w2b[:, fc, hf * 512:(hf + 1) * 512]) for fc in range(8)],
                            [("w2b",)] + [("Y", fc, t) for fc in range(8)], [PK(bo)])
                        tt(X[:, t, hf * 512:(hf + 1) * 512], X[:, t, hf * 512:(hf + 1) * 512], ps(bo), ALU.add,
                           [("X", t), PK(bo)], [("X", t)])
                    if last and fblk == 3:
                        dma("sp", out_d[t * 128:(t + 1) * 128, :], X[:, t, :], [("X", t)], [("out", t)])

        stopped = False
        for l in range(n_layers):
            phases = [("N", lambda: norm_phase(l, C_G1)), ("C", lambda: nsa_compress(l)), ("J", lambda: nsa_proj(l)),
                      ("T", lambda: nsa_attn(l)), ("A", lambda: gmlp(l)), ("V", lambda: conf(l)), ("P", lambda: sconv(l)),
                      ("G", lambda: gate_phase(l)), ("M", lambda: norm_phase(l, C_G2)),
                      ("F", lambda: ffn(l, (l == n_layers - 1) and debug_stop is None))]
            _wb0 = R(0, 4096, BF16).rearrange("p (k c) -> p k c", k=8)
            _wb1 = R(8192, 4096, BF16).rearrange("p (k c) -> p k c", k=8)
            wload(_wb0[:, :, 0:256], win_d[l, :, :, W_BF:W_BF + 256], [("wbuf", 0), ("w1b",)])
            wload(_wb1[:, :, 0:268], win_d[l, :, :, W_BT2:W_BT2 + 268], [("wbuf", 1), ("w1b",)])
            for nm, fnp in phases:
                fnp()
                if debug_stop == "%s%d" % (nm, l):
                    store_dbg(); stopped = True; break
            if stopped:
                break

        fin = T.final_tokens()

        with nc.Block() as block:
            def run(engname, e):
                for waits, fn, inc, dsem in T.ops[engname]:
                    for sid, v in waits:
                        e.wait_ge(sems[sid], v)
                    if dsem is not None:
                        fn(e, sems[dsem])
                    else:
                        ins = fn(e)
                        ins.then_inc(sems[inc[0]], inc[1])
                if engname == "sp":
                    for sid, v in fin.items():
                        if v > 0:
                            e.wait_ge(sems[sid], v)

            @block.tensor
            def _(e):
                run("pe", e)

            @block.scalar
            def _(e):
                run("act", e)

            @block.vector
            def _(e):
                run("dve", e)

            @block.gpsimd
            def _(e):
                run("pool", e)

            @block.sync
            def _(e):
                run("sp", e)
    return nc


def _overlap_matrix():
    cs = np.arange(127) * 16
    ce = cs + 32
    ss = np.arange(32) * 64
    se = ss + 64
    return ((cs[:, None] < se[None, :]) & (ce[:, None] > ss[None, :])).astype(np.float32)


def _host_inputs(inp):
    f = lambda a: np.ascontiguousarray(np.asarray(a, dtype=np.float32))
    bf = lambda a: np.ascontiguousarray(np.asarray(a, dtype=np.float32).astype(ml_dtypes.bfloat16))
    pp = np.zeros((128, L, NPP), np.float32)
    for l in range(L):
        pp[:, l, C_G1:C_G1 + 8] = f(inp["norm1_g"])[l].reshape(8, 128).T
        pp[:, l, C_G2:C_G2 + 8] = f(inp["norm2_g"])[l].reshape(8, 128).T
        pp[:, l, C_BS:C_BS + 4] = f(inp["gmlp_bs"])[l].T
        pp[:, l, C_GQ] = np.tile(f(inp["nsa_q_norm_g"])[l], 2)
        for i in range(3):
            pp[:, l, C_GK + i] = np.tile(f(inp["nsa_k_norm_g"])[l, i], 2)
        cw = f(inp["conf_conv_w"])[l]
        for cc in range(2):
            pp[:, l, C_CW + cc * 31:C_CW + (cc + 1) * 31] = cw[:, cc * 128:(cc + 1) * 128].T
        pp[:, l, C_CB:C_CB + 2] = f(inp["conf_conv_b"])[l].reshape(2, 128).T
        pp[:, l, C_CLG:C_CLG + 2] = f(inp["conf_ln_g"])[l].reshape(2, 128).T
        pp[:, l, C_CLB:C_CLB + 2] = f(inp["conf_ln_b"])[l].reshape(2, 128).T
        sw = f(inp["sconv_w"])[l]
        for cc in range(2):
            pp[:, l, C_SW + cc * 3:C_SW + (cc + 1) * 3] = sw[:, cc * 128:(cc + 1) * 128].T
        pp[:, l, C_BG:C_BG + 32] = f(inp["b_gate"])[l].reshape(32, 128).T
        pe = f(inp["nsa_cmp_pe"])[l]
        for kv in range(2):
            pp[:, l, C_PE + kv * 32:C_PE + (kv + 1) * 32] = np.tile(pe[kv].T, (2, 1))
        pp[:, l, C_LNG:C_LNG + 256] = np.broadcast_to(f(inp["gmlp_ln_g"])[l][None, :], (128, 256))
        pp[:, l, C_LNB:C_LNB + 256] = np.broadcast_to(f(inp["gmlp_ln_b"])[l][None, :], (128, 256))
    pp = pp.reshape(128, L * NPP)
    p = np.arange(128)[:, None]
    q = np.arange(128)[None, :]
    cb = np.zeros((128, 2560), np.float32)
    cb[:, 0:128] = (p == q)
    cb[:, 128:256] = (p <= q)
    cb[:, 256:384] = (p >= q)
    cb[:, 384:512] = (p > q)
    n = np.arange(128)[:, None]
    s = np.arange(S)[None, :]
    cb[:, 512:2560] = ((16 * n + 31) <= s) & (n < 127)
    eall = np.zeros((2, 128, S), np.float32)
    E = (np.arange(32)[:, None] == (np.arange(S)[None, :] // 64)).astype(np.float32)
    eall[0, 64:96, :] = E
    eall[1, 0:32, :] = E
    rconst = np.zeros((127, 33), np.float32)
    rconst[:, 0] = 1.0
    rconst[:, 1:33] = _overlap_matrix()
    blk = np.arange(32)[None, :]
    cur = (np.arange(S) // 64)[:, None]
    forced = (blk == 0) | ((cur - blk >= 0) & (cur - blk < 2))
    causal = blk <= cur
    fbias = np.where(forced, 100.0, np.where(causal, 0.0, -100.0)).astype(np.float32)
    fbias = fbias.reshape(NT, 128, 32).transpose(1, 0, 2).reshape(128, NT * 32)
    w_in = f(inp["w_in"])
    perm = np.concatenate([np.arange(0, 512), np.arange(512, 768), np.arange(1024, 1280), np.arange(1280, 1548),
                           np.arange(768, 1024), np.arange(1548, 2828)])
    win = w_in[:, :, perm].reshape(L, 8, 128, 2828).transpose(0, 2, 1, 3)
    wgate = f(inp["w_gate"]).reshape(L, 8, 128, 4, 8, 128)
    wg = wgate.transpose(0, 4, 2, 1, 3, 5).reshape(L, 8, 128, 8 * 4 * 128)
    wbr = f(inp["w_branch"]).reshape(L, 4, 2, 128, 8, 128)
    wb = wbr.transpose(0, 4, 3, 1, 2, 5).reshape(L, 8, 128, 8 * 128)
    wo = f(inp["w_out"]).reshape(L, 8, 128, 1024).transpose(0, 2, 1, 3)
    w1 = f(inp["w_mlp1"]).reshape(L, 8, 128, 4, 1024).transpose(0, 3, 2, 1, 4)
    w2 = f(inp["w_mlp2"]).reshape(L, 4, 8, 128, 1024).transpose(0, 1, 3, 2, 4)
    cw1 = f(inp["nsa_cmp_w1"]).reshape(L, 2, 32, 64, 128).transpose(0, 1, 3, 2, 4)
    cw1 = np.concatenate([cw1, cw1], axis=2)
    cw2 = f(inp["nsa_cmp_w2"]).transpose(0, 2, 1, 3).reshape(L, 128, 128)
    ws = f(inp["gmlp_ws"]).transpose(0, 2, 1, 3).reshape(L, 128, 512)
    c = np.ascontiguousarray
    return dict(pp=c(pp), cb16=bf(cb), eall=bf(eall), rconst=bf(rconst), fbias=c(fbias), win=c(win), wg=c(wg), wb=c(wb),
                wo=c(wo), w1=c(w1), w2=c(w2), cw1=c(cw1), cw2=c(cw2), ws=c(ws))


_NC_CACHE = {}


def kernel(**inputs):
    shared = _host_inputs(inputs)
    x = np.ascontiguousarray(np.asarray(inputs["x"], dtype=np.float32))
    B = x.shape[0]
    key = (N_LAYERS, DEBUG_STOP)
    nc = build_nc(N_LAYERS, DEBUG_STOP)
    in_maps = []
    for b in range(B):
        m = dict(shared)
        m["x"] = x[b]
        in_maps.append(m)
    res = run_bass_kernel_spmd(nc, in_maps, core_ids=list(range(B)))
    out = np.stack([np.asarray(r["out"], dtype=np.float32) for r in res.results], axis=0)
    if DEBUG_STOP is not None:
        kernel.dbg = [np.asarray(r["dbgy"]) for r in res.results]
    return out
```
